# Optimizing a Trainium2 kernel written in Bass

```python
import jax, jax.numpy as jnp
from jax import lax
import numpy as np

D_MODEL = 1024
BATCH = 8
SEQ = 4096
DEPTH = 2

ATT_HEADS = 8
ATT_KV_HEADS = 2
ATT_HEAD_DIM = 64
ATT_GROUP = ATT_HEADS // ATT_KV_HEADS
WINDOW = 128
ATT_BLOCK = 128
ROPE_DIM = ATT_HEAD_DIM // 4
ROPE_THETA = 500000.0
M_HEADS = 4
M_QK_DIM = 64
M_V_DIM = 128
M_CHUNK = 64
D_FF = 4 * D_MODEL
EPS = 1e-6

ATT_Q_W = ATT_HEADS * ATT_HEAD_DIM
ATT_KV_W = ATT_KV_HEADS * ATT_HEAD_DIM
M_QK_W = M_HEADS * M_QK_DIM
M_V_W = M_HEADS * M_V_DIM
IN_WIDTHS = (ATT_Q_W, ATT_KV_W, ATT_KV_W, M_QK_W, M_QK_W, M_V_W, M_V_W, 2 * M_HEADS, 2 * D_MODEL)
D_IN = sum(IN_WIDTHS)

kernel_name = "hybrid_swa_mlstm_gated_block"


def rmsnorm(x, gain):
    x32 = x.astype(jnp.float32)
    inv = lax.rsqrt(jnp.mean(x32 * x32, axis=-1, keepdims=True) + EPS)
    return (x32 * inv).astype(x.dtype) * gain


def rope_tables(seq_len):
    pos = jnp.arange(seq_len, dtype=jnp.float32)
    inv_freq = ROPE_THETA ** (-jnp.arange(0, ROPE_DIM, 2, dtype=jnp.float32) / ROPE_DIM)
    ang = pos[:, None] * inv_freq[None, :]
    return jnp.cos(ang), jnp.sin(ang)


def partial_rope(x, cos, sin):
    half = ROPE_DIM // 2
    x1 = x[..., :half].astype(jnp.float32)
    x2 = x[..., half:ROPE_DIM].astype(jnp.float32)
    c = cos[None, :, None, :]
    s = sin[None, :, None, :]
    return jnp.concatenate([(x1 * c - x2 * s).astype(x.dtype),
                            (x2 * c + x1 * s).astype(x.dtype),
                            x[..., ROPE_DIM:]], axis=-1)


def sliding_window_attention(q, k, v, sinks):
    B, S, _, hd = q.shape
    nb = S // ATT_BLOCK
    qb = q.reshape(B, nb, ATT_BLOCK, ATT_KV_HEADS, ATT_GROUP, hd)

    def with_prev(t):
        tb = t.reshape(B, nb, ATT_BLOCK, ATT_KV_HEADS, hd)
        prev = jnp.pad(tb, ((0, 0), (1, 0), (0, 0), (0, 0), (0, 0)))[:, :-1]
        return jnp.concatenate([prev, tb], axis=2)

    kb, vb = with_prev(k), with_prev(v)
    scores = jnp.einsum('bnqhgd,bnkhd->bnhgqk', qb, kb).astype(jnp.float32) * (hd ** -0.5)
    blk = jnp.arange(nb)[:, None, None]
    qpos = blk * ATT_BLOCK + jnp.arange(ATT_BLOCK)[None, :, None]
    kpos = (blk - 1) * ATT_BLOCK + jnp.arange(2 * ATT_BLOCK)[None, None, :]
    valid = (kpos >= 0) & (kpos <= qpos) & (qpos - kpos < WINDOW)
    scores = jnp.where(valid[None, :, None, None], scores, -jnp.inf)
    sink = sinks.astype(jnp.float32).reshape(ATT_KV_HEADS, ATT_GROUP)[None, None, :, :, None, None]
    m = jnp.maximum(jnp.max(scores, axis=-1, keepdims=True), sink)
    p = jnp.exp(scores - m)
    probs = (p / (jnp.sum(p, axis=-1, keepdims=True) + jnp.exp(sink - m))).astype(v.dtype)
    out = jnp.einsum('bnhgqk,bnkhd->bnqhgd', probs, vb)
    return out.reshape(B, S, ATT_HEADS * hd)


def mlstm_chunkwise(q, k, v, i_pre, f_pre):
    f32 = jnp.float32
    B, S, H, dk = q.shape
    dv = v.shape[-1]
    L = M_CHUNK
    nc = S // L

    def chunks(t):
        return t.astype(f32).reshape(B, nc, L, H, -1).transpose(0, 1, 3, 2, 4)

    q = chunks(q) * (dk ** -0.5)
    k = chunks(k)
    v = chunks(v)
    log_f = jax.nn.log_sigmoid(f_pre.astype(f32)).reshape(B, nc, L, H).transpose(0, 1, 3, 2)
    log_i = i_pre.astype(f32).reshape(B, nc, L, H).transpose(0, 1, 3, 2)
    b = jnp.cumsum(log_f, axis=-1)
    g = b[..., -1]

    w_end = g[..., None] - b + log_i
    m_loc = jnp.max(w_end, axis=-1)
    e = jnp.exp(w_end - m_loc[..., None])
    dC = jnp.einsum('bnhlv,bnhlk->bnhvk', v * e[..., None], k)
    dn = jnp.einsum('bnhl,bnhlk->bnhk', e, k)

    def step(carry, inp):
        C, n, m = carry
        dC_c, dn_c, g_c, m_loc_c = inp
        m_new = jnp.maximum(g_c + m, m_loc_c)
        a = jnp.exp(g_c + m - m_new)
        s = jnp.exp(m_loc_c - m_new)
        C_new = a[..., None, None] * C + s[..., None, None] * dC_c
        n_new = a[..., None] * n + s[..., None] * dn_c
        return (C_new, n_new, m_new), (C, n, m)

    init = (jnp.zeros((B, H, dv, dk), f32), jnp.zeros((B, H, dk), f32), jnp.zeros((B, H), f32))
    xs = (jnp.moveaxis(dC, 1, 0), jnp.moveaxis(dn, 1, 0), jnp.moveaxis(g, 1, 0), jnp.moveaxis(m_loc, 1, 0))
    _, (C_prev, n_prev, m_prev) = lax.scan(step, init, xs)
    C_prev = jnp.moveaxis(C_prev, 0, 1)
    n_prev = jnp.moveaxis(n_prev, 0, 1)
    m_prev = jnp.moveaxis(m_prev, 0, 1)

    causal = jnp.tril(jnp.ones((L, L), dtype=bool))
    log_d = jnp.where(causal, b[..., :, None] - b[..., None, :] + log_i[..., None, :], -jnp.inf)
    log_inter = b + m_prev[..., None]
    m_t = jnp.maximum(log_inter, jnp.max(log_d, axis=-1))
    w = jnp.exp(log_d - m_t[..., None]) * jnp.einsum('bnhlk,bnhsk->bnhls', q, k)
    a_inter = jnp.exp(log_inter - m_t)
    num = jnp.einsum('bnhls,bnhsv->bnhlv', w, v) + a_inter[..., None] * jnp.einsum('bnhvk,bnhlk->bnhlv', C_prev, q)
    den = jnp.sum(w, axis=-1) + a_inter * jnp.einsum('bnhk,bnhlk->bnhl', n_prev, q)
    h = num / jnp.maximum(jnp.abs(den), jnp.exp(-m_t))[..., None]
    return h.transpose(0, 1, 3, 2, 4).reshape(B, S, H, dv)


def hybrid_layer(x, cos, sin, norm_mix, w_in, att_q_norm, att_k_norm, att_sinks, m_gate_bias,
                 m_head_norm, w_att_branch, w_m_branch, w_out, norm_ffn, w_ff1, w_ff2):
    B, S, _ = x.shape
    h = rmsnorm(x, norm_mix)
    z = h @ w_in
    splits = [int(s) for s in np.cumsum(IN_WIDTHS)[:-1]]
    q_a, k_a, v_a, q_m, k_m, v_m, o_m, if_m, gates = jnp.split(z, splits, axis=-1)

    q_a = partial_rope(rmsnorm(q_a.reshape(B, S, ATT_HEADS, ATT_HEAD_DIM), att_q_norm), cos, sin)
    k_a = partial_rope(rmsnorm(k_a.reshape(B, S, ATT_KV_HEADS, ATT_HEAD_DIM), att_k_norm), cos, sin)
    v_a = v_a.reshape(B, S, ATT_KV_HEADS, ATT_HEAD_DIM)
    att = sliding_window_attention(q_a, k_a, v_a, att_sinks)

    i_pre, f_pre = jnp.split(if_m + m_gate_bias, 2, axis=-1)
    hm = mlstm_chunkwise(q_m.reshape(B, S, M_HEADS, M_QK_DIM),
                         k_m.reshape(B, S, M_HEADS, M_QK_DIM),
                         v_m.reshape(B, S, M_HEADS, M_V_DIM), i_pre, f_pre).astype(x.dtype)
    hm = rmsnorm(hm, m_head_norm.reshape(M_HEADS, M_V_DIM)).reshape(B, S, M_V_W)
    hm = hm * jax.nn.sigmoid(o_m)

    g_a, g_m = jnp.split(gates, 2, axis=-1)
    mixed = jax.nn.sigmoid(g_a) * (att @ w_att_branch) + jax.nn.sigmoid(g_m) * (hm @ w_m_branch)
    x = x + mixed @ w_out

    u = jax.nn.relu(rmsnorm(x, norm_ffn) @ w_ff1)
    return x + (u * u) @ w_ff2


def setup_inputs(seed: int = 0) -> dict:
    key = jax.random.key(seed)
    ks = jax.random.split(key, 16)
    nrm = jax.random.normal
    f32 = jnp.float32
    x = nrm(ks[0], (BATCH, SEQ, D_MODEL), f32)
    norm_mix = 1.0 + 0.02 * nrm(ks[1], (DEPTH, D_MODEL), f32)
    w_in = nrm(ks[2], (DEPTH, D_MODEL, D_IN), f32) * D_MODEL ** -0.5
    att_q_norm = 1.0 + 0.02 * nrm(ks[3], (DEPTH, ATT_HEAD_DIM), f32)
    att_k_norm = 1.0 + 0.02 * nrm(ks[4], (DEPTH, ATT_HEAD_DIM), f32)
    att_sinks = 0.5 * nrm(ks[5], (DEPTH, ATT_HEADS), f32)
    i_bias = -1.0 + 0.1 * nrm(ks[6], (DEPTH, M_HEADS), f32)
    f_bias = 3.0 + 0.5 * nrm(ks[7], (DEPTH, M_HEADS), f32)
    m_gate_bias = jnp.concatenate([i_bias, f_bias], axis=-1)
    m_head_norm = 1.0 + 0.02 * nrm(ks[8], (DEPTH, M_V_W), f32)
    w_att_branch = nrm(ks[9], (DEPTH, ATT_Q_W, D_MODEL), f32) * ATT_Q_W ** -0.5
    w_m_branch = nrm(ks[10], (DEPTH, M_V_W, D_MODEL), f32) * M_V_W ** -0.5
    w_out = nrm(ks[11], (DEPTH, D_MODEL, D_MODEL), f32) * D_MODEL ** -0.5
    norm_ffn = 1.0 + 0.02 * nrm(ks[12], (DEPTH, D_MODEL), f32)
    w_ff1 = nrm(ks[13], (DEPTH, D_MODEL, D_FF), f32) * D_MODEL ** -0.5
    w_ff2 = nrm(ks[14], (DEPTH, D_FF, D_MODEL), f32) * D_FF ** -0.5
    return {"x": x, "norm_mix": norm_mix, "w_in": w_in, "att_q_norm": att_q_norm,
            "att_k_norm": att_k_norm, "att_sinks": att_sinks, "m_gate_bias": m_gate_bias,
            "m_head_norm": m_head_norm, "w_att_branch": w_att_branch, "w_m_branch": w_m_branch,
            "w_out": w_out, "norm_ffn": norm_ffn, "w_ff1": w_ff1, "w_ff2": w_ff2}


def reference(x, norm_mix, w_in, att_q_norm, att_k_norm, att_sinks, m_gate_bias, m_head_norm,
              w_att_branch, w_m_branch, w_out, norm_ffn, w_ff1, w_ff2):
    cos, sin = rope_tables(x.shape[1])
    for layer in range(DEPTH):
        x = hybrid_layer(x, cos, sin, norm_mix[layer], w_in[layer], att_q_norm[layer],
                         att_k_norm[layer], att_sinks[layer], m_gate_bias[layer],
                         m_head_norm[layer], w_att_branch[layer], w_m_branch[layer],
                         w_out[layer], norm_ffn[layer], w_ff1[layer], w_ff2[layer])
    return x
```

```python
import contextlib
import math
import numpy as np
import concourse.bass as bass
import concourse.mybir as mybir
from concourse.bass_utils import run_bass_kernel_spmd

F32 = mybir.dt.float32
BF16 = mybir.dt.bfloat16
AF = mybir.ActivationFunctionType
ALU = mybir.AluOpType
AX = mybir.AxisListType

S = 4096
D = 1024
T = 512
NT = S // T
KC = 8
NL = 2
NPIECE = 30
NB = 4
EPS = 1e-6
SEM_LIM = 12000
import os
USE_POOL = bool(int(os.environ.get('KPOOL', '1')))
MAGIC = 12582912.0
TWO_PI = 2.0 * math.pi

C_QA, C_KA, C_IFI, C_IFF, C_QM, C_KM, C_OM, C_GA, C_GM = 0, 4, 5, 6, 7, 9, 11, 15, 23

K_ONES, K_BLK, K_ROT, K_MOWN, K_MPREV, K_ID4, K_INVF, K_AR, K_DIAG, K_IDENT = 0, 128, 256, 384, 512, 640, 644, 645, 1157, 1669
NCST = 1797
P_GMIX, P_GFFN, P_GQ, P_GK, P_SINK, P_GB, P_MHN = 0, 8, 16, 17, 18, 26, 28
PL = 32


class Buf:
    __slots__ = ("name", "w", "rd", "excl")

    def __init__(self, name, excl=False):
        self.name = name
        self.w = None
        self.rd = {}
        self.excl = excl


class Slot:
    __slots__ = ("sem", "count", "last")

    def __init__(self, sem):
        self.sem = sem
        self.count = 0
        self.last = None


class Op:
    __slots__ = ("eng", "emit", "deps", "need_inc", "ev", "slot", "idx")


class _Rec:
    def __init__(self):
        self.call = None

    def __getattr__(self, name):
        def f(*args, **kw):
            self.call = (name, args, kw)
            return None
        return f


class Prog:
    def __init__(self, nc, es):
        self.nc = nc
        self.es = es
        self.ops = []
        self.engs = {"pe": nc.tensor, "act": nc.scalar, "dve": nc.vector, "pool": nc.gpsimd, "sp": nc.sync}
        self.nsem = 0

    def new_sem(self, tag):
        self.nsem += 1
        return self.es.enter_context(self.nc.semaphore("s_%s_%d" % (tag, self.nsem)))

    def new_slot(self, tag):
        return Slot(self.new_sem(tag))

    def op(self, eng, emit, reads=(), writes=(), slot=None):
        if eng == "pool" and slot is None and not USE_POOL:
            eng = "dve"
        o = Op()
        o.eng = eng
        rec = _Rec()
        emit(rec)
        name, args, kw = rec.call
        o.emit = lambda E: getattr(E, name)(*args, **kw)
        o.slot = slot
        o.need_inc = False
        o.ev = None
        o.idx = len(self.ops)
        deps = {}
        for b in reads:
            if b.w is not None:
                deps[b.w.idx] = b.w
            if b.excl:
                for r in b.rd.values():
                    deps[r.idx] = r
        for b in writes:
            if b.w is not None:
                deps[b.w.idx] = b.w
            for r in b.rd.values():
                deps[r.idx] = r
        if slot is not None and slot.last is not None:
            deps[slot.last.idx] = slot.last
        deps.pop(o.idx, None)
        dl = []
        for d in deps.values():
            if d.slot is None and d.eng == "pe" and eng == "pe" and slot is None:
                continue
            d.need_inc = True
            dl.append(d)
        o.deps = dl
        key = eng if slot is None else ("dma", id(slot))
        for b in reads:
            b.rd[key] = o
        for b in writes:
            b.w = o
            b.rd = {}
        if slot is not None:
            slot.last = o
        self.ops.append(o)
        return o

    def I(self, eng, name, *args, rd=(), wr=(), slot=None, **kw):
        return self.op(eng, lambda E: getattr(E, name)(*args, **kw), reads=rd, writes=wr, slot=slot)

    def emit_all(self, final_waits):
        nc = self.nc
        seen = {k: {} for k in self.engs}
        ticks = {k: 0 for k in self.engs}
        sems = {k: [] for k in self.engs}
        for o in self.ops:
            E = self.engs[o.eng]
            sn = seen[o.eng]
            for d in o.deps:
                sem, val = d.ev
                k = id(sem)
                if sn.get(k, 0) < val:
                    E.wait_ge(sem, val)
                    sn[k] = val
            inst = o.emit(E)
            if o.slot is not None:
                inst.then_inc(o.slot.sem, 16)
                o.slot.count += 16
                o.ev = (o.slot.sem, o.slot.count)
            elif o.need_inc:
                t = ticks[o.eng]
                ticks[o.eng] = t + 1
                gi, v = divmod(t, SEM_LIM)
                if gi >= len(sems[o.eng]):
                    sems[o.eng].append(self.new_sem(o.eng))
                sem = sems[o.eng][gi]
                inst.then_inc(sem, 1)
                o.ev = (sem, v + 1)
        E = self.engs["sp"]
        for o in final_waits:
            sem, val = o.ev
            E.wait_ge(sem, val)


class _Stop(Exception):
    pass


def build_program(ntiles=NT, nlayers=NL, stage=99):
    nc = bass.Bass("TRN2", target_bir_lowering=False)
    es = contextlib.ExitStack()
    P = Prog(nc, es)

    xT = nc.dram_tensor("xT", [D, S], F32, kind="ExternalInput").ap()
    wall = nc.dram_tensor("wall", [NL * NPIECE, 128, 4096], F32, kind="ExternalInput").ap()
    par_d = nc.dram_tensor("par", [128, NL * PL], F32, kind="ExternalInput").ap()
    cst_d = nc.dram_tensor("cst", [128, NCST], F32, kind="ExternalInput").ap()
    yT = nc.dram_tensor("yT", [D, S], F32, kind="ExternalOutput").ap()
    wbf = nc.dram_tensor("wbf", [NL * NPIECE, 128, 4096], BF16, kind="Internal").ap()

    def SB(name, shape, dt=F32):
        return es.enter_context(nc.sbuf_tensor(name, shape, dt))

    xbuf = [SB("xbuf%d" % i, [128, KC, T]) for i in range(2)]
    hA = SB("hA", [128, KC, T], BF16)
    R = SB("R", [128, 32, T], BF16)
    sqt = SB("sqt", [128, 2, T], BF16)
    NTMP = 5
    tmpf = SB("tmpf", [128, NTMP, T])
    kTb = [SB("kTb%d" % l, [128, 128 + T], BF16) for l in range(NL)]
    vaug = [SB("vaug%d" % l, [128, 5, 2, 128], BF16) for l in range(NL)]
    vmaug = SB("vmaug", [128, 4, 4, 256], BF16)
    ktok = SB("ktok", [128, 4, 256], BF16)
    PT = SB("PT", [128, 4, T], BF16)
    attT = SB("attT", [128, 4, T], BF16)
    hmT = SB("hmT", [128, 4, T], BF16)
    cosT = SB("cosT", [128, T])
    sinT = SB("sinT", [128, T])
    Cst = [SB("Cst%d" % l, [128, 4, 256]) for l in range(NL)]
    Cbf = SB("Cbf", [128, 4, 256], BF16)
    WT = SB("WT", [128, 2, 4, 128], BF16)
    kp = SB("kp", [128, 2, 4, 64], BF16)
    NG = 6
    gt = SB("gt", [128, NG, T])
    gones = SB("gones", [128, T])
    gsm = SB("gsm", [128, 64])
    gsb = SB("gsb", [128, 4, 8])
    carry = [SB("carry%d" % l, [128, 8]) for l in range(NL)]
    wr = SB("wr", [128, NB, 4096], BF16)
    cst = SB("cst_sb", [128, NCST])
    cstb = SB("cstb", [128, 768], BF16)
    maskb = SB("maskb", [128, 2, 512], BF16)
    qkf = SB("qkf", [128, 4, T])
    qkb = SB("qkb", [128, 4, T], BF16)
    par = SB("par_sb", [128, NL * PL])
    par2 = SB("par2", [128, NL * 16])
    psum = [es.enter_context(nc.psum_tensor("ps%d" % i, [128, 512], F32)) for i in range(8)]

    def bufs(name, n):
        return [Buf("%s%d" % (name, i)) for i in range(n)]

    b_x = [bufs("x%d_" % i, KC) for i in range(2)]
    b_hA = bufs("hA", KC)
    b_R = bufs("R", 32)
    b_sq = bufs("sq", 2)
    b_tmp = bufs("tmp", NTMP)
    b_kT = [[Buf("kTc%d" % l), Buf("kTm%d" % l)] for l in range(NL)]
    b_va = [bufs("va%d_" % l, 5) for l in range(NL)]
    b_vm = bufs("vm", 4)
    b_kt = bufs("kt", 4)
    b_PT = bufs("PT", 4)
    b_att = [[Buf("att%d_%d" % (b, j)) for j in range(2)] for b in range(4)]
    b_hm = bufs("hm", 4)
    b_rope = Buf("rope")
    b_Cst = [bufs("Cst%d_" % l, 4) for l in range(NL)]
    b_Cbf = bufs("Cbf", 4)
    b_WT = bufs("WT", 2)
    b_kp = bufs("kp", 2)
    b_gt = bufs("gt", NG)
    b_gsm = Buf("gsm")
    b_qkf = bufs("qkf", 4)
    b_qkb = bufs("qkb", 4)
    b_gones = Buf("gones")
    b_gsb = bufs("gsb", 4)
    b_carry = [Buf("carry%d" % l) for l in range(NL)]
    b_wr = bufs("wr", NB)
    b_cst = Buf("cst")
    b_cstb = Buf("cstb")
    b_par = Buf("par")
    b_par2 = Buf("par2")
    b_ps = [Buf("ps%d" % i, excl=True) for i in range(8)]
    b_wbf = bufs("wbf", NL * NPIECE)

    st = {"ps": 0, "tmp": 0, "sq": 0, "gt": 0}

    held = set()

    def next_ps(hold=False):
        while st["ps"] % 8 in held:
            st["ps"] += 1
        i = st["ps"] % 8
        st["ps"] += 1
        if hold:
            held.add(i)
        return psum[i], b_ps[i]

    def release_ps(ps):
        held.discard(psum.index(ps))

    def next_tmp():
        i = st["tmp"] % NTMP
        st["tmp"] += 1
        return tmpf[:, i, :], b_tmp[i]

    def next_sq():
        i = st["sq"] % 2
        st["sq"] += 1
        return sqt[:, i, :], b_sq[i]

    def next_gt(fixed=None):
        if fixed is not None:
            return gt[:, fixed, :], b_gt[fixed]
        i = 3 + st["gt"] % (NG - 3)
        st["gt"] += 1
        return gt[:, i, :], b_gt[i]

    sl_c = P.new_slot("cst")
    sl_p = P.new_slot("par")
    P.op("sp", lambda E: E.dma_start(out=cst[:], in_=cst_d), writes=[b_cst], slot=sl_c)
    P.op("sp", lambda E: E.dma_start(out=par[:], in_=par_d), writes=[b_par], slot=sl_p)
    cast_slots = [P.new_slot("cast") for _ in range(NL * NPIECE)]
    def cast_piece(i):
        P.op("pool", lambda E: E.dma_start(out=wbf[i], in_=wall[i]), writes=[b_wbf[i]], slot=cast_slots[i])

    for i in range(NPIECE):
        cast_piece(i)
    late_casts = list(range(NPIECE, nlayers * NPIECE))

    def emit_late_casts(n):
        for _ in range(min(n, len(late_casts))):
            cast_piece(late_casts.pop(0))
    P.op("dve", lambda E: E.tensor_copy(out=cstb[:, 0:640], in_=cst[:, 0:640]), reads=[b_cst], writes=[b_cstb])
    for l in range(NL):
        P.op("act", (lambda l: lambda E: E.activation(out=par2[:, l * 16:l * 16 + 8],
                                                      in_=par[:, l * PL + P_SINK:l * PL + P_SINK + 8], func=AF.Exp))(l),
             reads=[b_par], writes=[b_par2])
        P.op("dve", (lambda l: lambda E: E.tensor_scalar(out=par2[:, l * 16 + 8:l * 16 + 9],
                                                         in0=par[:, l * PL + P_GB + 1:l * PL + P_GB + 2],
                                                         scalar1=-1.0, scalar2=None, op0=ALU.mult))(l),
             reads=[b_par], writes=[b_par2])
        P.op("dve", (lambda l: lambda E: E.memset(Cst[l][:], 0.0))(l), writes=b_Cst[l])
        P.op("dve", (lambda l: lambda E: E.memset(carry[l][:], 0.0))(l), writes=[b_carry[l]])
        P.op("dve", (lambda l: lambda E: E.memset(vaug[l][:], 1.0))(l), writes=b_va[l])
        P.op("dve", (lambda l: lambda E: E.memset(kTb[l][:], 0.0))(l), writes=b_kT[l])
    P.op("dve", lambda E: E.memset(vmaug[:], 1.0), writes=b_vm)
    P.op("dve", lambda E: E.memset(gt[:], 0.0), writes=b_gt)
    P.op("dve", lambda E: E.memset(gsm[:], 0.0), writes=[b_gsm])
    P.op("dve", lambda E: E.memset(gones[:], 1.0), writes=[b_gones])

    P.op("dve", lambda E: E.tensor_copy(out=cstb[:, 640:768], in_=cst[:, K_IDENT:K_IDENT + 128]), reads=[b_cst], writes=[b_cstb])
    for m_, off_ in ((0, K_MOWN), (1, K_MPREV)):
        for g_ in range(4):
            P.op("dve", lambda E: E.tensor_scalar(out=maskb[:, m_, g_ * 128:(g_ + 1) * 128], in0=cst[:, off_:off_ + 128],
                                                  scalar1=-1.0, scalar2=30000.0, op0=ALU.add, op1=ALU.mult), reads=[b_cst], writes=[b_cstb])
    ident_bf = cstb[:, 640:768]
    ones_bf = cstb[:, K_ONES:K_ONES + 128]
    blk_bf = cstb[:, K_BLK:K_BLK + 128]
    rot_bf = cstb[:, K_ROT:K_ROT + 128]
    mown_bf = cstb[:, K_MOWN:K_MOWN + 128]
    mprev_bf = cstb[:, K_MPREV:K_MPREV + 128]

    ring_slots = [P.new_slot("ring") for _ in range(NB)]
    seq = [(t, l, p) for t in range(ntiles) for l in range(nlayers) for p in range(NPIECE)]
    ring = {"issued": 0}

    def piece(n):
        while ring["issued"] < min(n + NB - 1, len(seq)):
            m = ring["issued"]
            (_, l, p) = seq[m]
            s = m % NB
            P.op("sp", (lambda s, i: lambda E: E.dma_start(out=wr[:, s, :], in_=wbf[i]))(s, l * NPIECE + p),
                 reads=[b_wbf[l * NPIECE + p]], writes=[b_wr[s]], slot=ring_slots[s])
            ring["issued"] += 1
        return n % NB, b_wr[n % NB]

    x_slots = [P.new_slot("xin") for _ in range(2)]
    o_slots = [P.new_slot("xout") for _ in range(2)]
    out_ops = []

    def load_x(t):
        xb = t % 2
        P.op("sp", lambda E: E.dma_start(out=xbuf[xb][:],
                                         in_=xT[:, t * T:(t + 1) * T].rearrange("(kc p) n -> p kc n", p=128)),
             writes=b_x[xb], slot=x_slots[xb])

    def rmsnorm(xb, gcol):
        X = xbuf[xb]
        ps, bps = next_ps()
        for kc in range(KC):
            sq, bsq = next_sq()
            P.op("act", (lambda kc, sq: lambda E: E.activation(out=sq, in_=X[:, kc, :], func=AF.Square))(kc, sq),
                 reads=[b_x[xb][kc]], writes=[bsq])
            P.op("pe", (lambda kc, sq: lambda E: E.matmul(ps[:], lhsT=ones_bf, rhs=sq, start=(kc == 0), stop=(kc == KC - 1)))(kc, sq),
                 reads=[bsq, b_cstb], writes=[bps])
        rs, brs = next_tmp()
        P.op("act", lambda E: E.activation(out=rs, in_=ps[:], func=AF.Ln, scale=1.0 / D, bias=EPS), reads=[bps], writes=[brs])
        P.op("act", lambda E: E.activation(out=rs, in_=rs, func=AF.Exp, scale=-0.5), reads=[brs], writes=[brs])
        for kc in range(KC):
            P.op("dve", (lambda kc: lambda E: E.scalar_tensor_tensor(
                out=hA[:, kc, :], in0=X[:, kc, :], scalar=par[:, gcol + kc:gcol + kc + 1], in1=rs,
                op0=ALU.mult, op1=ALU.mult))(kc),
                reads=[b_x[xb][kc], brs, b_par], writes=[b_hA[kc]])

    def rope_tables(t):
        a1, ba1 = next_tmp()
        a2, ba2 = next_tmp()
        rd = [b_cst]
        P.op("dve", lambda E: E.tensor_scalar(out=a1, in0=cst[:, K_AR:K_AR + T], scalar1=float(t * T),
                                              scalar2=cst[:, K_INVF:K_INVF + 1], op0=ALU.add, op1=ALU.mult),
             reads=rd, writes=[ba1])
        for (tab, shift) in ((sinT, 0.0), (cosT, 0.25)):
            P.op("dve", (lambda shift: lambda E: E.tensor_scalar(out=a2, in0=a1, scalar1=1.0 / TWO_PI, scalar2=shift,
                                                                 op0=ALU.mult, op1=ALU.add))(shift),
                 reads=[ba1], writes=[ba2])
            k1, bk1 = next_tmp()
            P.op("dve", (lambda k1: lambda E: E.tensor_scalar(out=k1, in0=a2, scalar1=MAGIC, scalar2=None, op0=ALU.add))(k1),
                 reads=[ba2], writes=[bk1])
            P.op("dve", (lambda k1: lambda E: E.tensor_scalar(out=k1, in0=k1, scalar1=-MAGIC, scalar2=None, op0=ALU.add))(k1),
                 reads=[bk1], writes=[bk1])
            P.op("dve", (lambda k1: lambda E: E.tensor_tensor(out=k1, in0=a2, in1=k1, op=ALU.subtract))(k1),
                 reads=[ba2, bk1], writes=[bk1])
            P.op("act", (lambda k1, tab: lambda E: E.activation(out=tab[:], in_=k1, func=AF.Sin, scale=TWO_PI))(k1, tab),
                 reads=[bk1], writes=[b_rope])

    pend = []
    tk = {"n": 0, "qk": 0}

    def defer(delay, fn):
        pend.append([tk["n"] + delay, fn])

    def tick():
        tk["n"] += 1
        run = [p for p in pend if p[0] <= tk["n"]]
        for p in run:
            pend.remove(p)
        for p in run:
            p[1]()

    def flush():
        while pend:
            tick()

    def qk_process(ps, bps, l, gcol, out_ap, out_bufs):
        s_ = tk["qk"] % 2
        tk["qk"] += 1
        q32, bq = qkf[:, 2 * s_, :], b_qkf[2 * s_]
        rs, brs = qkf[:, 2 * s_ + 1, :], b_qkf[2 * s_ + 1]
        sq, bsq = qkb[:, 2 * s_, :], b_qkb[2 * s_]
        qb, bqb = qkb[:, 2 * s_ + 1, :], b_qkb[2 * s_ + 1]
        P.op("act", lambda E: E.activation(out=sq, in_=ps[:], func=AF.Square), reads=[bps], writes=[bsq])
        P.op("act", lambda E: E.activation(out=q32, in_=ps[:], func=AF.Copy), reads=[bps], writes=[bq])

        def s2():
            ps2, bps2 = next_ps()
            P.op("pe", lambda E: E.matmul(ps2[:], lhsT=blk_bf, rhs=sq, start=True, stop=True), reads=[bsq, b_cstb], writes=[bps2])
            P.op("act", lambda E: E.activation(out=rs, in_=ps2[:], func=AF.Ln, scale=1.0 / 64.0, bias=EPS), reads=[bps2], writes=[brs])
            P.op("act", lambda E: E.activation(out=rs, in_=rs, func=AF.Exp, scale=-0.5), reads=[brs], writes=[brs])
            P.op("dve", lambda E: E.scalar_tensor_tensor(out=q32, in0=q32, scalar=par[:, gcol:gcol + 1], in1=rs,
                                                         op0=ALU.mult, op1=ALU.mult), reads=[bq, brs, b_par], writes=[bq])
            P.op("act", lambda E: E.activation(out=qb, in_=q32, func=AF.Copy), reads=[bq], writes=[bqb])

            def s3():
                ps3, bps3 = next_ps()
                P.op("pe", lambda E: E.matmul(ps3[:], lhsT=rot_bf, rhs=qb, start=True, stop=True), reads=[bqb, b_cstb], writes=[bps3])
                P.op("dve", lambda E: E.tensor_tensor(out=rs, in0=ps3[:], in1=sinT[:], op=ALU.mult), reads=[bps3, b_rope], writes=[brs])
                P.op("pool", lambda E: E.tensor_tensor(out=q32, in0=q32, in1=cosT[:], op=ALU.mult), reads=[bq, b_rope], writes=[bq])
                P.op("dve", lambda E: E.tensor_tensor(out=out_ap, in0=q32, in1=rs, op=ALU.add), reads=[bq, brs], writes=out_bufs)
            defer(1, s3)
        defer(1, s2)

    def dense_chunk(n, ci, rhs_list, rhs_bufs, evac):
        s, bw = piece(n)
        nk = len(rhs_list)
        ps, bps = next_ps()
        for kc in range(nk):
            P.op("pe", lambda E: E.matmul(
                ps[:], lhsT=wr[:, s, kc * 512 + ci * 128:kc * 512 + ci * 128 + 128], rhs=rhs_list[kc],
                start=(kc == 0), stop=(kc == nk - 1)),
                reads=[bw, rhs_bufs[kc]], writes=[bps])
        tick()
        evac(ci, ps, bps)

    def dense_fm(n, nchunks, rhs_list, rhs_bufs, evac):
        for ci in range(nchunks):
            dense_chunk(n, ci, rhs_list, rhs_bufs, evac)

    cur = {}

    def chk(k):
        if stage == k:
            raise _Stop()

    def main_loop():
        load_x(0)
        for t in range(ntiles):
            xb = t % 2
            cur['t'], cur['xb'] = t, xb
            chk(0)
            if t + 1 < ntiles:
                load_x(t + 1)
            rope_tables(t)
            for l in range(nlayers):
                base = (t * nlayers + l) * NPIECE
                pc = l * PL
                hrhs = [hA[:, kc, :] for kc in range(KC)]
                rmsnorm(xb, pc + P_GMIX)
                chk(1)
                emit_late_casts(8)

                gI = [None]
                gF = [None]

                def evac_win(cidx):
                    def f(ci, ps, bps):
                        c = cidx + ci
                        import os
                        dbg = os.environ.get("KDBG", "qifms")
                        kind = "q" if c <= C_KA else "i" if c == C_IFI else "f" if c == C_IFF else "m" if c < C_OM else "s"
                        if kind not in dbg:
                            P.op("act", lambda E: E.activation(out=R[:, c, :], in_=ps[:], func=AF.Copy), reads=[bps], writes=[b_R[c]])
                            return
                        if c < 4:
                            qk_process(ps, bps, l, pc + P_GQ, R[:, c, :], [b_R[c]])
                        elif c == C_KA:
                            qk_process(ps, bps, l, pc + P_GK, kTb[l][:, 128:128 + T], [b_kT[l][1]])
                        elif c == C_IFI:
                            g, bg = next_gt(0)
                            gI[0] = (g, bg)
                            P.op("dve", lambda E: E.tensor_scalar(out=g[0:4, :], in0=ps[0:4, :], scalar1=par[0:4, pc + P_GB:pc + P_GB + 1],
                                                                  scalar2=None, op0=ALU.add), reads=[bps, b_par], writes=[bg])
                        elif c == C_IFF:
                            g, bg = next_gt(1)
                            gF[0] = (g, bg)
                            P.op("act", lambda E: E.activation(out=g[0:4, :], in_=ps[0:4, :], func=AF.Exp, scale=-1.0,
                                                               bias=par2[0:4, l * 16 + 8:l * 16 + 9]), reads=[bps, b_par2], writes=[bg])
                        elif c in (C_QM, C_QM + 1):
                            P.op("act", lambda E: E.activation(out=R[:, c, :], in_=ps[:], func=AF.Copy, scale=0.125), reads=[bps], writes=[b_R[c]])
                        elif c in (C_KM, C_KM + 1):
                            P.op("dve", lambda E: E.tensor_copy(out=R[:, c, :], in_=ps[:]), reads=[bps], writes=[b_R[c]])
                        elif c < 31:
                            sg, bsg = next_tmp()
                            P.op("act", lambda E: E.activation(out=sg, in_=ps[:], func=AF.Exp, scale=-1.0), reads=[bps], writes=[bsg])
                            P.op("act", lambda E: E.activation(out=sg, in_=sg, func=AF.Ln, bias=1.0), reads=[bsg], writes=[bsg])
                            P.op("act", lambda E: E.activation(out=R[:, c, :], in_=sg, func=AF.Exp, scale=-1.0), reads=[bsg], writes=[b_R[c]])
                    return f

                for pp in range(3):
                    dense_fm(base + pp, 4, hrhs, b_hA, evac_win(pp * 4))
                dn_steps = []
                for pp in range(3, 8):
                    for ci in range(4 if pp < 7 else 3):
                        dn_steps.append((lambda pp, ci: lambda: dense_chunk(base + pp + 2, ci, hrhs, b_hA, evac_win(pp * 4)))(pp, ci))

                flush()
                chk(2)
                s8, bw8 = piece(base + 3)
                for sub in range(4):
                    ps, bps = next_ps()
                    for kc in range(KC):
                        P.op("pe", (lambda sub, kc, ps: lambda E: E.matmul(
                            ps[:], lhsT=hA[:, kc, sub * 128:(sub + 1) * 128], rhs=wr[:, s8, kc * 512:(kc + 1) * 512],
                            start=(kc == 0), stop=(kc == KC - 1)))(sub, kc, ps),
                            reads=[bw8, b_hA[kc]], writes=[bps])
                    P.op("act", (lambda sub, ps: lambda E: E.activation(
                        out=vmaug[:, sub, :, 0:128], in_=ps[:].rearrange("p (h v) -> p h v", h=4), func=AF.Copy))(sub, ps),
                        reads=[bps], writes=[b_vm[sub]])
                s9, bw9 = piece(base + 4)
                for sub in range(4):
                    ps, bps = next_ps()
                    for kc in range(KC):
                        P.op("pe", (lambda sub, kc, ps: lambda E: E.matmul(
                            ps[:, 0:384], lhsT=hA[:, kc, sub * 128:(sub + 1) * 128], rhs=wr[:, s9, kc * 512:kc * 512 + 384],
                            start=(kc == 0), stop=(kc == KC - 1)))(sub, kc, ps),
                            reads=[bw9, b_hA[kc]], writes=[bps])
                    P.op("dve", (lambda sub, ps: lambda E: E.tensor_copy(out=vaug[l][:, 1 + sub, 0, 0:64], in_=ps[:, 0:64]))(sub, ps),
                         reads=[bps], writes=[b_va[l][1 + sub]])
                    P.op("dve", (lambda sub, ps: lambda E: E.tensor_copy(out=vaug[l][:, 1 + sub, 1, 64:128], in_=ps[:, 64:128]))(sub, ps),
                         reads=[bps], writes=[b_va[l][1 + sub]])
                    P.op("act", (lambda sub, ps: lambda E: E.activation(out=ktok[:, sub, :], in_=ps[:, 128:384], func=AF.Copy))(sub, ps),
                         reads=[bps], writes=[b_kt[sub]])

                chk(3)
                def att_scores(b, j):
                    gb = t * 4 + b
                    nr = slice(j * 64, j * 64 + 64)
                    kbs = ([(b, 1)] if gb > 0 else []) + [(b + 1, 0)]
                    pts = []
                    for ki, (blk, mi) in enumerate(kbs):
                        ps, bps = next_ps()
                        P.op("pe", lambda E: E.matmul(ps[:], lhsT=ident_bf, rhs=maskb[:, mi, :], start=True, stop=False),
                             reads=[b_cstb], writes=[bps])
                        for g in range(4):
                            P.op("pe", lambda E: E.matmul(
                                ps[:, g * 128:(g + 1) * 128], lhsT=kTb[l][nr, blk * 128:(blk + 1) * 128],
                                rhs=R[nr, g, b * 128:(b + 1) * 128], start=False, stop=(g == 3)),
                                reads=[b_kT[l][0 if blk == 0 else 1], b_R[g]], writes=[bps])
                        pi = (j * 2 + ki)
                        P.op("act", lambda E: E.activation(out=PT[:, pi, :], in_=ps[:], func=AF.Exp, scale=0.125),
                             reads=[bps], writes=[b_PT[pi]])
                        pts.append((pi, blk))
                    return pts

                def att_pv(b, j, pts):
                    nr = slice(j * 64, j * 64 + 64)
                    dr = slice((1 - j) * 64, (1 - j) * 64 + 64)
                    pv, bpv = next_ps()
                    for ki, (pi, blk) in enumerate(pts):
                        P.op("pe", lambda E: E.matmul(
                            pv[:], lhsT=vaug[l][:, blk, j, :], rhs=PT[:, pi, :], start=(ki == 0), stop=(ki == len(pts) - 1)),
                            reads=[b_va[l][blk], b_PT[pi]], writes=[bpv])
                    dt_, bdt = next_tmp()
                    P.op("dve", lambda E: E.tensor_tensor(
                        out=dt_[nr, :].rearrange("p (g q) -> p g q", g=4), in0=pv[dr, :].rearrange("p (g q) -> p g q", g=4),
                        in1=par2[dr, l * 16 + j * 4:l * 16 + j * 4 + 4].unsqueeze(2).to_broadcast([64, 4, 128]), op=ALU.add),
                        reads=[bpv, b_par2], writes=[bdt])
                    P.op("act", lambda E: E.activation(out=dt_[nr, :], in_=dt_[nr, :], func=AF.Ln), reads=[bdt], writes=[bdt])
                    P.op("act", lambda E: E.activation(out=dt_[nr, :], in_=dt_[nr, :], func=AF.Exp, scale=-1.0), reads=[bdt], writes=[bdt])
                    P.op("dve", lambda E: E.tensor_tensor(
                        out=attT[nr, :, b * 128:(b + 1) * 128], in0=pv[nr, :].rearrange("p (g q) -> p g q", g=4),
                        in1=dt_[nr, :].rearrange("p (g q) -> p g q", g=4), op=ALU.mult),
                        reads=[bpv, bdt], writes=[b_att[b][j]])

                at_steps = []
                at_state = {}
                items = [(b, j) for b in range(4) for j in range(2)]
                for ii, (b, j) in enumerate(items):
                    at_steps.append((lambda b, j: lambda: at_state.__setitem__((b, j), att_scores(b, j)))(b, j))
                    if ii > 0:
                        pb, pj = items[ii - 1]
                        at_steps.append((lambda pb, pj: lambda: att_pv(pb, pj, at_state[(pb, pj)]))(pb, pj))
                pb, pj = items[-1]
                at_steps.append((lambda pb, pj: lambda: att_pv(pb, pj, at_state[(pb, pj)]))(pb, pj))

                def att_tail():
                    P.op("pool", lambda E: E.tensor_copy(out=kTb[l][:, 0:128], in_=kTb[l][:, T:T + 128]), reads=[b_kT[l][1]], writes=[b_kT[l][0]])
                    P.op("pool", lambda E: E.tensor_copy(out=vaug[l][:, 0, :, :], in_=vaug[l][:, 4, :, :]), reads=[b_va[l][4]], writes=[b_va[l][0]])

                chk(4)
                gi, bgi = gI[0]
                ge, bge = gF[0]
                cr = carry[l]
                P.op("act", lambda E: E.activation(out=ge[0:4, :], in_=ge[0:4, :], func=AF.Ln, bias=1.0), reads=[bge], writes=[bge])
                Bg, bBg = next_gt(2)
                P.op("dve", lambda E: E.tensor_tensor_scan(out=Bg[0:4, :], data0=gones[0:4, :],
                                                           data1=ge[0:4, :], initial=cr[0:4, 0:1], op0=ALU.mult, op1=ALU.subtract),
                     reads=[bge, b_carry[l], b_gones], writes=[bBg])
                P.op("dve", lambda E: E.tensor_copy(out=cr[0:4, 0:1], in_=Bg[0:4, T - 1:T]), reads=[bBg], writes=[b_carry[l]])
                P.op("dve", lambda E: E.tensor_tensor(out=gi[0:4, :], in0=gi[0:4, :], in1=Bg[0:4, :], op=ALU.subtract), reads=[bgi, bBg], writes=[bgi])
                P.op("dve", lambda E: E.tensor_copy(out=gsm[0:4, 0:1], in_=cr[0:4, 1:2]), reads=[b_carry[l]], writes=[b_gsm])
                P.op("dve", lambda E: E.tensor_reduce(out=gsm[0:4, 8:12], in_=gi[0:4, :].rearrange("p (c s) -> p c s", c=4), axis=AX.X, op=ALU.max),
                     reads=[bgi, b_gsm], writes=[b_gsm])
                P.op("dve", lambda E: E.tensor_tensor_scan(out=gsm[0:4, 1:5], data0=gsm[0:4, 8:12], data1=gsm[0:4, 8:12], initial=gsm[0:4, 0:1],
                                                           op0=ALU.max, op1=ALU.max), reads=[b_gsm], writes=[b_gsm])
                P.op("dve", lambda E: E.tensor_copy(out=cr[0:4, 1:2], in_=gsm[0:4, 4:5]), reads=[b_gsm], writes=[b_carry[l]])
                P.op("dve", lambda E: E.tensor_tensor(out=gsm[0:4, 24:28], in0=gsm[0:4, 0:4], in1=gsm[0:4, 1:5], op=ALU.subtract), reads=[b_gsm], writes=[b_gsm])
                P.op("act", lambda E: E.activation(out=gsm[0:4, 16:20], in_=gsm[0:4, 24:28], func=AF.Exp), reads=[b_gsm], writes=[b_gsm])
                Mb = gsm[0:4, 1:5].unsqueeze(2).to_broadcast([4, 4, 128])
                P.op("dve", lambda E: E.tensor_tensor(out=gi[0:4, :].rearrange("p (c s) -> p c s", c=4), in0=gi[0:4, :].rearrange("p (c s) -> p c s", c=4),
                                                      in1=Mb, op=ALU.subtract), reads=[bgi, b_gsm], writes=[bgi])
                P.op("act", lambda E: E.activation(out=gi[0:4, :], in_=gi[0:4, :], func=AF.Exp), reads=[bgi], writes=[bgi])
                P.op("dve", lambda E: E.scalar_tensor_tensor(out=Bg[0:4, :].rearrange("p (c s) -> p c s", c=4), in0=Bg[0:4, :].rearrange("p (c s) -> p c s", c=4),
                                                             scalar=-1.0, in1=Mb, op0=ALU.mult, op1=ALU.subtract), reads=[bBg, b_gsm], writes=[bBg])
                P.op("act", lambda E: E.activation(out=Bg[0:4, :], in_=Bg[0:4, :], func=AF.Exp), reads=[bBg], writes=[bBg])

                chk(5)
                mst = {}

                def mA(c):
                    cc = slice(c * 128, (c + 1) * 128)
                    wi = c % 2
                    td, btd = next_gt()
                    P.op("dve", lambda E: E.tensor_tensor(
                        out=td[0:4, :].rearrange("p (h s) -> p h s", h=4), in0=cst[0:4, K_DIAG:K_DIAG + 512].rearrange("p (h s) -> p h s", h=4),
                        in1=Bg[0:4, cc].unsqueeze(1).to_broadcast([4, 4, 128]), op=ALU.mult), reads=[bBg, b_cst], writes=[btd])
                    P.op("dve", lambda E: E.tensor_scalar(out=gsm[0:4, 32:36], in0=cst[0:4, K_ID4:K_ID4 + 4], scalar1=gsm[0:4, 16 + c:17 + c],
                                                          scalar2=None, op0=ALU.mult), reads=[b_gsm, b_cst], writes=[b_gsm])
                    psg, bpsg = next_ps()
                    P.op("pe", lambda E: E.matmul(psg[:, 0:4], lhsT=gi[0:4, cc], rhs=cst[0:4, K_ID4:K_ID4 + 4], start=True, stop=True),
                         reads=[bgi, b_cst], writes=[bpsg])
                    P.op("pe", lambda E: E.matmul(psg[:, 4:8], lhsT=cst[0:4, K_ONES:K_ONES + 128], rhs=gsm[0:4, 32:36], start=True, stop=True),
                         reads=[b_gsm, b_cst], writes=[bpsg])
                    P.op("dve", lambda E: E.tensor_copy(out=gsb[:, c, :], in_=psg[:, 0:8]), reads=[bpsg], writes=[b_gsb[c]])
                    psS2 = [next_ps(), next_ps()]
                    for h in range(4):
                        hp = slice((h % 2) * 64, (h % 2) * 64 + 64)
                        psS, bpsS = psS2[h % 2]
                        co = (h // 2) * 128
                        P.op("pe", lambda E: E.matmul(
                            psS[:, co:co + 128], lhsT=R[hp, C_KM + h // 2, cc], rhs=R[hp, C_QM + h // 2, cc], start=True, stop=True),
                            reads=[b_R[C_KM + h // 2], b_R[C_QM + h // 2]], writes=[bpsS])
                    for h in range(4):
                        psS, bpsS = psS2[h % 2]
                        co = (h // 2) * 128
                        P.op("dve", lambda E: E.scalar_tensor_tensor(
                            out=WT[:, wi, h, :], in0=psS[:, co:co + 128], scalar=gsb[:, c, h:h + 1], in1=cst[:, K_MOWN:K_MOWN + 128],
                            op0=ALU.mult, op1=ALU.mult), reads=[bpsS, b_gsb[c], b_cst], writes=[b_WT[wi]])
                        P.op("pool", lambda E: E.tensor_scalar(
                            out=kp[:, wi, h, :], in0=ktok[:, c, h * 64:(h + 1) * 64], scalar1=gsb[:, c, h:h + 1], scalar2=None, op0=ALU.mult),
                            reads=[b_kt[c], b_gsb[c]], writes=[b_kp[wi]])
                    mst[c] = {"td": (td, btd)}

                def mB(c):
                    cc = slice(c * 128, (c + 1) * 128)
                    wi = c % 2
                    for h in range(4):
                        hp = slice((h % 2) * 64, (h % 2) * 64 + 64)
                        P.op("dve", lambda E: E.tensor_scalar(out=Cbf[hp, h, :], in0=Cst[l][hp, h, :], scalar1=gsb[hp, c, 4 + h:5 + h],
                                                              scalar2=None, op0=ALU.mult),
                             reads=[b_Cst[l][h], b_gsb[c]], writes=[b_Cbf[h]])
                    psN, bpsN = next_ps(hold=True)
                    psD, bpsD = next_ps(hold=True)
                    for (pq, bpq, off) in ((psN, bpsN, 0), (psD, bpsD, 128)):
                        for h in range(4):
                            hp = slice((h % 2) * 64, (h % 2) * 64 + 64)
                            P.op("pe", lambda E: E.matmul(
                                pq[:, h * 128:(h + 1) * 128], lhsT=vmaug[:, c, h, off:off + 128], rhs=WT[:, wi, h, :], start=True, stop=False),
                                reads=[b_vm[c], b_WT[wi]], writes=[bpq])
                            P.op("pe", lambda E: E.matmul(
                                pq[:, h * 128:(h + 1) * 128], lhsT=Cbf[hp, h, off:off + 128], rhs=R[hp, C_QM + h // 2, cc], start=False, stop=True),
                                reads=[b_Cbf[h], b_R[C_QM + h // 2]], writes=[bpq])
                    for hh in range(2):
                        psC, bpsC = next_ps()
                        for h2 in range(2):
                            h = hh * 2 + h2
                            P.op("pe", lambda E: E.matmul(
                                psC[0:64, h2 * 256:(h2 + 1) * 256], lhsT=kp[:, wi, h, :], rhs=vmaug[:, c, h, :], start=True, stop=True),
                                reads=[b_kp[wi], b_vm[c]], writes=[bpsC])
                        for h2 in range(2):
                            h = hh * 2 + h2
                            hp = slice((h % 2) * 64, (h % 2) * 64 + 64)
                            P.op("dve", lambda E: E.scalar_tensor_tensor(
                                out=Cst[l][hp, h, :], in0=Cst[l][hp, h, :], scalar=gsb[hp, c, 4 + h:5 + h], in1=psC[0:64, h2 * 256:(h2 + 1) * 256],
                                op0=ALU.mult, op1=ALU.add), reads=[b_Cst[l][h], b_gsb[c], bpsC], writes=[b_Cst[l][h]])
                    mst[c]["nd"] = (psN, bpsN, psD, bpsD)

                def mC(c):
                    cc = slice(c * 128, (c + 1) * 128)
                    td, btd = mst[c]["td"]
                    psN, bpsN, psD, bpsD = mst[c]["nd"]
                    pst, bpst = next_ps()
                    P.op("pe", lambda E: E.matmul(pst[:], lhsT=cst[0:4, K_ONES:K_ONES + 128], rhs=td[0:4, :], start=True, stop=True),
                         reads=[btd, b_cst], writes=[bpst])
                    thrb, bthr = next_tmp()
                    P.op("dve", lambda E: E.tensor_copy(out=thrb, in_=pst[:]), reads=[bpst], writes=[bthr])
                    ad, bad = next_tmp()
                    P.op("act", lambda E: E.activation(out=ad, in_=psD[:], func=AF.Abs), reads=[bpsD], writes=[bad])
                    P.op("dve", lambda E: E.tensor_tensor(out=ad, in0=ad, in1=thrb, op=ALU.max), reads=[bad, bthr], writes=[bad])
                    P.op("act", lambda E: E.activation(out=ad, in_=ad, func=AF.Ln), reads=[bad], writes=[bad])
                    P.op("act", lambda E: E.activation(out=ad, in_=ad, func=AF.Exp, scale=-1.0), reads=[bad], writes=[bad])
                    P.op("dve", lambda E: E.tensor_tensor(out=ad, in0=psN[:], in1=ad, op=ALU.mult), reads=[bpsN, bad], writes=[bad])
                    sq, bsq = next_sq()
                    P.op("act", lambda E: E.activation(out=sq, in_=ad, func=AF.Square), reads=[bad], writes=[bsq])
                    pss, bpss = next_ps()
                    P.op("pe", lambda E: E.matmul(pss[:], lhsT=ones_bf, rhs=sq, start=True, stop=True), reads=[bsq, b_cstb], writes=[bpss])
                    rs, brs = next_tmp()
                    P.op("act", lambda E: E.activation(out=rs, in_=pss[:], func=AF.Ln, scale=1.0 / 128.0, bias=EPS), reads=[bpss], writes=[brs])
                    P.op("act", lambda E: E.activation(out=rs, in_=rs, func=AF.Exp, scale=-0.5), reads=[brs], writes=[brs])
                    P.op("pool", lambda E: E.tensor_tensor(out=ad, in0=ad, in1=rs, op=ALU.mult), reads=[bad, brs], writes=[bad])
                    P.op("pool", lambda E: E.tensor_tensor(
                        out=ad.rearrange("p (h s) -> p h s", h=4), in0=ad.rearrange("p (h s) -> p h s", h=4),
                        in1=par[:, pc + P_MHN:pc + P_MHN + 4].unsqueeze(2).to_broadcast([128, 4, 128]), op=ALU.mult), reads=[bad, b_par], writes=[bad])
                    P.op("pool", lambda E: E.tensor_tensor(
                        out=hmT[:, :, cc], in0=ad.rearrange("p (h s) -> p h s", h=4), in1=R[:, C_OM:C_OM + 4, cc], op=ALU.mult),
                        reads=[bad] + b_R[C_OM:C_OM + 4], writes=[b_hm[c]])
                    release_ps(psN)
                    release_ps(psD)

                ml_steps = []
                for (stg, c) in (("A", 0), ("A", 1), ("B", 0), ("A", 2), ("C", 0), ("B", 1), ("A", 3), ("C", 1), ("B", 2), ("C", 2), ("B", 3), ("C", 3)):
                    ml_steps.append((lambda stg, c: lambda: {"A": mA, "B": mB, "C": mC}[stg](c))(stg, c))
                merged = []
                for lst in (at_steps, ml_steps, dn_steps):
                    for i_, fn_ in enumerate(lst):
                        merged.append(((i_ + 0.5) / len(lst), len(merged), fn_))
                merged.sort(key=lambda x: (x[0], x[1]))
                emit_late_casts(6)
                for k_, (_, _, fn_) in enumerate(merged):
                    fn_()
                    if k_ % 4 == 3:
                        emit_late_casts(1)
                emit_late_casts(len(late_casts))
                att_tail()
                flush()
                chk(6)
                sA, bwA = piece(base + 10)
                sM, bwM = piece(base + 11)
                att_reads = [x for bb in b_att for x in bb]
                for oc in range(8):
                    psA, bpsA = next_ps()
                    for g in range(4):
                        P.op("pe", (lambda oc, g, psA: lambda E: E.matmul(psA[:], lhsT=wr[:, sA, g * 1024 + oc * 128:g * 1024 + oc * 128 + 128], rhs=attT[:, g, :],
                                                                          start=(g == 0), stop=(g == 3)))(oc, g, psA), reads=[bwA] + att_reads, writes=[bpsA])
                    psB, bpsB = next_ps()
                    for h in range(4):
                        P.op("pe", (lambda oc, h, psB: lambda E: E.matmul(psB[:], lhsT=wr[:, sM, h * 1024 + oc * 128:h * 1024 + oc * 128 + 128], rhs=hmT[:, h, :],
                                                                          start=(h == 0), stop=(h == 3)))(oc, h, psB), reads=[bwM] + b_hm, writes=[bpsB])
                    t1, bt1 = next_tmp()
                    t2, bt2 = next_tmp()
                    P.op("dve", (lambda oc, psA, t1: lambda E: E.tensor_tensor(out=t1, in0=psA[:], in1=R[:, C_GA + oc, :], op=ALU.mult))(oc, psA, t1),
                         reads=[bpsA, b_R[C_GA + oc]], writes=[bt1])
                    P.op("dve", (lambda oc, psB, t2: lambda E: E.tensor_tensor(out=t2, in0=psB[:], in1=R[:, C_GM + oc, :], op=ALU.mult))(oc, psB, t2),
                         reads=[bpsB, b_R[C_GM + oc]], writes=[bt2])
                    kbr = os.environ.get("KBR", "")
                    if kbr == "a":
                        P.op("dve", (lambda oc, t1, t2: lambda E: E.tensor_copy(out=hA[:, oc, :], in_=t1))(oc, t1, t2), reads=[bt1, bt2], writes=[b_hA[oc]])
                    elif kbr == "m":
                        P.op("dve", (lambda oc, t1, t2: lambda E: E.tensor_copy(out=hA[:, oc, :], in_=t2))(oc, t1, t2), reads=[bt1, bt2], writes=[b_hA[oc]])
                    else:
                        P.op("pool", (lambda oc, t1, t2: lambda E: E.tensor_tensor(out=hA[:, oc, :], in0=t1, in1=t2, op=ALU.add))(oc, t1, t2),
                             reads=[bt1, bt2], writes=[b_hA[oc]])

                chk(7)
                X = xbuf[xb]
                for half in range(2):
                    def evac_out(ci, ps, bps, half=half):
                        oc = half * 4 + ci
                        P.op("dve", lambda E: E.tensor_tensor(out=X[:, oc, :], in0=X[:, oc, :], in1=ps[:], op=ALU.add),
                             reads=[bps, b_x[xb][oc]], writes=[b_x[xb][oc]])
                    dense_fm(base + 12 + half, 4, hrhs, b_hA, evac_out)

                chk(8)
                rmsnorm(xb, pc + P_GFFN)
                for pp in range(8):
                    def evac_ff1(ci, ps, bps, pp=pp):
                        c = pp * 4 + ci
                        tr, btr = next_tmp()
                        P.op("act", lambda E: E.activation(out=tr, in_=ps[:], func=AF.Relu), reads=[bps], writes=[btr])
                        P.op("pool" if (c % 2 == 1) else "dve", lambda E: E.tensor_tensor(out=R[:, c, :], in0=tr, in1=tr, op=ALU.mult), reads=[btr], writes=[b_R[c]])
                    dense_fm(base + 14 + pp, 4, hrhs, b_hA, evac_ff1)
                last = (l == nlayers - 1)
                for oc in range(8):
                    s, bw = piece(base + 22 + oc)
                    ps, bps = next_ps()
                    for kc in range(32):
                        P.op("pe", (lambda kc, ps, s: lambda E: E.matmul(ps[:], lhsT=wr[:, s, kc * 128:(kc + 1) * 128], rhs=R[:, kc, :],
                                                                         start=(kc == 0), stop=(kc == 31)))(kc, ps, s), reads=[bw, b_R[kc]], writes=[bps])
                    P.op("dve", (lambda oc, ps: lambda E: E.tensor_tensor(out=X[:, oc, :], in0=X[:, oc, :], in1=ps[:], op=ALU.add))(oc, ps),
                         reads=[bps, b_x[xb][oc]], writes=[b_x[xb][oc]])
            out_ops.append(P.op("sp", (lambda t, xb: lambda E: E.dma_start(
                out=yT[:, t * T:(t + 1) * T].rearrange("(kc p) n -> p kc n", p=128), in_=xbuf[xb][:]))(t, xb),
                reads=b_x[xb], slot=o_slots[xb]))


    try:
        main_loop()
    except _Stop:
        t_, xb_ = cur['t'], cur['xb']
        out_ops.append(P.op('sp', lambda E: E.dma_start(out=yT[:, t_ * T:(t_ + 1) * T].rearrange('(kc p) n -> p kc n', p=128), in_=xbuf[xb_][:]), reads=b_x[xb_], slot=o_slots[xb_]))

    P.emit_all(out_ops)
    es.close()
    return nc


def _consts():
    c = np.zeros((128, NCST), np.float32)
    c[:, K_ONES:K_ONES + 128] = 1.0
    p = np.arange(128)
    c[:, K_BLK:K_BLK + 128] = (p[:, None] // 64 == p[None, :] // 64)
    rot = np.zeros((128, 128), np.float32)
    for hb in (0, 64):
        for d in range(8):
            rot[hb + d + 8, hb + d] = -1.0
            rot[hb + d, hb + d + 8] = 1.0
    c[:, K_ROT:K_ROT + 128] = rot
    c[:, K_MOWN:K_MOWN + 128] = (p[:, None] <= p[None, :])
    c[:, K_MPREV:K_MPREV + 128] = (p[:, None] > p[None, :])
    c[0:4, K_ID4:K_ID4 + 4] = np.eye(4, dtype=np.float32)
    d = p % 64
    invf = np.where(d < 16, (500000.0 ** (-(2.0 * (d % 8)) / 16.0)), 0.0)
    c[:, K_INVF] = invf.astype(np.float32)
    c[:, K_AR:K_AR + T] = np.arange(T, dtype=np.float32)[None, :]
    dg = np.zeros((4, 4, 128), np.float32)
    for h in range(4):
        dg[h, h, :] = 1.0
    c[0:4, K_DIAG:K_DIAG + 512] = dg.reshape(4, 512)
    c[:, K_IDENT:K_IDENT + 128] = np.eye(128, dtype=np.float32)
    return c


def _pieces(w_in, w_att, w_m, w_out, w1, w2):
    out = np.zeros((NPIECE, 128, 4096), np.float32)
    o_qa, o_ka, o_va, o_qm, o_km, o_vm, o_om, o_if, o_ga, o_gm = 0, 512, 640, 768, 1024, 1280, 1792, 2304, 2312, 3336
    cols = np.full((32, 128), -1, np.int64)
    for c in range(4):
        cols[C_QA + c, 0:64] = o_qa + c * 64 + np.arange(64)
        cols[C_QA + c, 64:128] = o_qa + (4 + c) * 64 + np.arange(64)
    cols[C_KA] = o_ka + np.arange(128)
    cols[C_IFI, 0:4] = o_if + np.arange(4)
    cols[C_IFF, 0:4] = o_if + 4 + np.arange(4)
    for c in range(2):
        cols[C_QM + c] = o_qm + c * 128 + np.arange(128)
        cols[C_KM + c] = o_km + c * 128 + np.arange(128)
    for c in range(4):
        cols[C_OM + c] = o_om + c * 128 + np.arange(128)
    for c in range(8):
        cols[C_GA + c] = o_ga + c * 128 + np.arange(128)
        cols[C_GM + c] = o_gm + c * 128 + np.arange(128)
    wpad = np.concatenate([w_in, np.zeros((D, 1), np.float32)], axis=1)
    fm = wpad[:, cols.reshape(-1)]
    fm = fm.reshape(KC, 128, 8, 512).transpose(2, 1, 0, 3)
    fm = fm.reshape(8, 128, 4096)
    out[0:3] = fm[0:3]
    out[5:10] = fm[3:8]
    out[3] = w_in[:, o_vm:o_vm + 512].reshape(KC, 128, 512).transpose(1, 0, 2).reshape(128, 4096)
    tm = np.zeros((D, 512), np.float32)
    tm[:, 0:128] = w_in[:, o_va:o_va + 128]
    tm[:, 128:384] = w_in[:, o_km:o_km + 256]
    out[4] = tm.reshape(KC, 128, 512).transpose(1, 0, 2).reshape(128, 4096)
    rows = np.zeros((4, 128), np.int64)
    for g in range(4):
        for j in range(2):
            rows[g, j * 64:(j + 1) * 64] = (j * 4 + g) * 64 + np.arange(64)
    out[10] = w_att[rows.reshape(-1)].reshape(4, 128, 1024).transpose(1, 0, 2).reshape(128, 4096)
    out[11] = w_m.reshape(4, 128, 1024).transpose(1, 0, 2).reshape(128, 4096)
    for half in range(2):
        out[12 + half] = w_out[:, half * 512:(half + 1) * 512].reshape(KC, 128, 512).transpose(1, 0, 2).reshape(128, 4096)
    for pp in range(8):
        out[14 + pp] = w1[:, pp * 512:(pp + 1) * 512].reshape(KC, 128, 512).transpose(1, 0, 2).reshape(128, 4096)
    for oc in range(8):
        out[22 + oc] = w2[:, oc * 128:(oc + 1) * 128].reshape(32, 128, 128).transpose(1, 0, 2).reshape(128, 4096)
    return out


def _params(norm_mix, norm_ffn, att_q_norm, att_k_norm, att_sinks, m_gate_bias, m_head_norm):
    par = np.zeros((128, NL * PL), np.float32)
    p = np.arange(128)
    for l in range(NL):
        b = l * PL
        par[:, b + P_GMIX:b + P_GMIX + 8] = norm_mix[l].reshape(8, 128).T
        par[:, b + P_GFFN:b + P_GFFN + 8] = norm_ffn[l].reshape(8, 128).T
        par[:, b + P_GQ] = att_q_norm[l][p % 64]
        par[:, b + P_GK] = att_k_norm[l][p % 64]
        par[:, b + P_SINK:b + P_SINK + 8] = att_sinks[l][None, :]
        par[0:4, b + P_GB] = m_gate_bias[l][0:4]
        par[0:4, b + P_GB + 1] = m_gate_bias[l][4:8]
        par[:, b + P_MHN:b + P_MHN + 4] = m_head_norm[l].reshape(4, 128).T
    return par


_NC_CACHE = {}


def kernel(x, norm_mix, w_in, att_q_norm, att_k_norm, att_sinks, m_gate_bias, m_head_norm,
           w_att_branch, w_m_branch, w_out, norm_ffn, w_ff1, w_ff2):
    f = lambda a: np.asarray(a, dtype=np.float32)
    x = f(x)
    B = x.shape[0]
    wall = np.concatenate([_pieces(f(w_in[l]), f(w_att_branch[l]), f(w_m_branch[l]), f(w_out[l]), f(w_ff1[l]), f(w_ff2[l]))
                           for l in range(NL)], axis=0)
    par = _params(f(norm_mix), f(norm_ffn), f(att_q_norm), f(att_k_norm), f(att_sinks), f(m_gate_bias), f(m_head_norm))
    cst = _consts()
    if "nc" not in _NC_CACHE:
        _NC_CACHE["nc"] = build_program()
    nc = _NC_CACHE["nc"]
    in_maps = [{"xT": np.ascontiguousarray(x[b].T), "wall": wall, "par": par, "cst": cst} for b in range(B)]
    res = run_bass_kernel_spmd(nc, in_maps, core_ids=list(range(B)))
    out = np.stack([np.ascontiguousarray(res.results[b]["yT"].T) for b in range(B)], axis=0)
    return out.astype(np.float32)
```

```python
import contextlib
import math
import numpy as np
import concourse.bass as bass
import concourse.mybir as mybir
from concourse.bass_utils import run_bass_kernel_spmd

F32 = mybir.dt.float32
BF16 = mybir.dt.bfloat16
AF = mybir.ActivationFunctionType
ALU = mybir.AluOpType
AX = mybir.AxisListType

S = 4096
D = 1024
T = 512
NT = S // T
KC = 8
NL = 2
NPIECE = 30
NB = 4
EPS = 1e-6
SEM_LIM = 12000
import os
USE_POOL = bool(int(os.environ.get('KPOOL', '1')))
MAGIC = 12582912.0
TWO_PI = 2.0 * math.pi

C_QA, C_KA, C_IFI, C_IFF, C_QM, C_KM, C_OM, C_GA, C_GM = 0, 4, 5, 6, 7, 9, 11, 15, 23

K_ONES, K_BLK, K_ROT, K_MOWN, K_MPREV, K_ID4, K_INVF, K_AR, K_DIAG, K_IDENT = 0, 128, 256, 384, 512, 640, 644, 645, 1157, 1669
NCST = 1797
P_GMIX, P_GFFN, P_GQ, P_GK, P_SINK, P_GB, P_MHN = 0, 8, 16, 17, 18, 26, 28
PL = 32


class Buf:
    __slots__ = ("name", "w", "rd", "excl")

    def __init__(self, name, excl=False):
        self.name = name
        self.w = None
        self.rd = {}
        self.excl = excl


class Slot:
    __slots__ = ("sem", "count", "last")

    def __init__(self, sem):
        self.sem = sem
        self.count = 0
        self.last = None


class Op:
    __slots__ = ("eng", "emit", "deps", "need_inc", "ev", "slot", "idx")


class _Rec:
    def __init__(self):
        self.call = None

    def __getattr__(self, name):
        def f(*args, **kw):
            self.call = (name, args, kw)
            return None
        return f


class Prog:
    def __init__(self, nc, es):
        self.nc = nc
        self.es = es
        self.ops = []
        self.engs = {"pe": nc.tensor, "act": nc.scalar, "dve": nc.vector, "pool": nc.gpsimd, "sp": nc.sync}
        self.nsem = 0

    def new_sem(self, tag):
        self.nsem += 1
        return self.es.enter_context(self.nc.semaphore("s_%s_%d" % (tag, self.nsem)))

    def new_slot(self, tag):
        return Slot(self.new_sem(tag))

    def op(self, eng, emit, reads=(), writes=(), slot=None):
        if eng == "pool" and slot is None and not USE_POOL:
            eng = "dve"
        o = Op()
        o.eng = eng
        rec = _Rec()
        emit(rec)
        name, args, kw = rec.call
        o.emit = lambda E: getattr(E, name)(*args, **kw)
        o.slot = slot
        o.need_inc = False
        o.ev = None
        o.idx = len(self.ops)
        deps = {}
        for b in reads:
            if b.w is not None:
                deps[b.w.idx] = b.w
            if b.excl:
                for r in b.rd.values():
                    deps[r.idx] = r
        for b in writes:
            if b.w is not None:
                deps[b.w.idx] = b.w
            for r in b.rd.values():
                deps[r.idx] = r
        if slot is not None and slot.last is not None:
            deps[slot.last.idx] = slot.last
        deps.pop(o.idx, None)
        dl = []
        for d in deps.values():
            if d.slot is None and d.eng == "pe" and eng == "pe" and slot is None:
                continue
            d.need_inc = True
            dl.append(d)
        o.deps = dl
        key = eng if slot is None else ("dma", id(slot))
        for b in reads:
            b.rd[key] = o
        for b in writes:
            b.w = o
            b.rd = {}
        if slot is not None:
            slot.last = o
        self.ops.append(o)
        return o

    def I(self, eng, name, *args, rd=(), wr=(), slot=None, **kw):
        return self.op(eng, lambda E: getattr(E, name)(*args, **kw), reads=rd, writes=wr, slot=slot)

    def emit_all(self, final_waits):
        nc = self.nc
        seen = {k: {} for k in self.engs}
        ticks = {k: 0 for k in self.engs}
        sems = {k: [] for k in self.engs}
        for o in self.ops:
            E = self.engs[o.eng]
            sn = seen[o.eng]
            for d in o.deps:
                sem, val = d.ev
                k = id(sem)
                if sn.get(k, 0) < val:
                    E.wait_ge(sem, val)
                    sn[k] = val
            inst = o.emit(E)
            if o.slot is not None:
                inst.then_inc(o.slot.sem, 16)
                o.slot.count += 16
                o.ev = (o.slot.sem, o.slot.count)
            elif o.need_inc:
                t = ticks[o.eng]
                ticks[o.eng] = t + 1
                gi, v = divmod(t, SEM_LIM)
                if gi >= len(sems[o.eng]):
                    sems[o.eng].append(self.new_sem(o.eng))
                sem = sems[o.eng][gi]
                inst.then_inc(sem, 1)
                o.ev = (sem, v + 1)
        E = self.engs["sp"]
        for o in final_waits:
            sem, val = o.ev
            E.wait_ge(sem, val)


class _Stop(Exception):
    pass


def build_program(ntiles=NT, nlayers=NL, stage=99):
    nc = bass.Bass("TRN2", target_bir_lowering=False)
    es = contextlib.ExitStack()
    P = Prog(nc, es)

    xT = nc.dram_tensor("xT", [D, S], F32, kind="ExternalInput").ap()
    wall = nc.dram_tensor("wall", [NL * NPIECE, 128, 4096], F32, kind="ExternalInput").ap()
    par_d = nc.dram_tensor("par", [128, NL * PL], F32, kind="ExternalInput").ap()
    cst_d = nc.dram_tensor("cst", [128, NCST], F32, kind="ExternalInput").ap()
    yT = nc.dram_tensor("yT", [D, S], F32, kind="ExternalOutput").ap()
    wbf = nc.dram_tensor("wbf", [NL * NPIECE, 128, 4096], BF16, kind="Internal").ap()

    def SB(name, shape, dt=F32):
        return es.enter_context(nc.sbuf_tensor(name, shape, dt))

    xbuf = [SB("xbuf%d" % i, [128, KC, T]) for i in range(2)]
    hA = SB("hA", [128, KC, T], BF16)
    R = SB("R", [128, 32, T], BF16)
    sqt = SB("sqt", [128, 2, T], BF16)
    NTMP = 5
    tmpf = SB("tmpf", [128, NTMP, T])
    kTb = [SB("kTb%d" % l, [128, 128 + T], BF16) for l in range(NL)]
    vaug = [SB("vaug%d" % l, [128, 5, 2, 128], BF16) for l in range(NL)]
    vmaug = SB("vmaug", [128, 4, 4, 256], BF16)
    ktok = SB("ktok", [128, 4, 256], BF16)
    PT = SB("PT", [128, 4, T], BF16)
    attT = SB("attT", [128, 4, T], BF16)
    hmT = SB("hmT", [128, 4, T], BF16)
    cosT = SB("cosT", [128, T])
    sinT = SB("sinT", [128, T])
    Cst = [SB("Cst%d" % l, [128, 4, 256]) for l in range(NL)]
    Cbf = SB("Cbf", [128, 4, 256], BF16)
    WT = SB("WT", [128, 2, 4, 128], BF16)
    kp = SB("kp", [128, 2, 4, 64], BF16)
    NG = 6
    gt = SB("gt", [128, NG, T])
    gones = SB("gones", [128, T])
    gsm = SB("gsm", [128, 64])
    gsb = SB("gsb", [128, 4, 8])
    carry = [SB("carry%d" % l, [128, 8]) for l in range(NL)]
    wr = SB("wr", [128, NB, 4096], BF16)
    cst = SB("cst_sb", [128, NCST])
    cstb = SB("cstb", [128, 768], BF16)
    maskb = SB("maskb", [128, 2, 512], BF16)
    qkf = SB("qkf", [128, 4, T])
    qkb = SB("qkb", [128, 4, T], BF16)
    par = SB("par_sb", [128, NL * PL])
    par2 = SB("par2", [128, NL * 16])
    psum = [es.enter_context(nc.psum_tensor("ps%d" % i, [128, 512], F32)) for i in range(8)]

    def bufs(name, n):
        return [Buf("%s%d" % (name, i)) for i in range(n)]

    b_x = [bufs("x%d_" % i, KC) for i in range(2)]
    b_hA = bufs("hA", KC)
    b_R = bufs("R", 32)
    b_sq = bufs("sq", 2)
    b_tmp = bufs("tmp", NTMP)
    b_kT = [[Buf("kTc%d" % l), Buf("kTm%d" % l)] for l in range(NL)]
    b_va = [bufs("va%d_" % l, 5) for l in range(NL)]
    b_vm = bufs("vm", 4)
    b_kt = bufs("kt", 4)
    b_PT = bufs("PT", 4)
    b_att = [[Buf("att%d_%d" % (b, j)) for j in range(2)] for b in range(4)]
    b_hm = bufs("hm", 4)
    b_rope = Buf("rope")
    b_Cst = [bufs("Cst%d_" % l, 4) for l in range(NL)]
    b_Cbf = bufs("Cbf", 4)
    b_WT = bufs("WT", 2)
    b_kp = bufs("kp", 2)
    b_gt = bufs("gt", NG)
    b_gsm = Buf("gsm")
    b_qkf = bufs("qkf", 4)
    b_qkb = bufs("qkb", 4)
    b_gones = Buf("gones")
    b_gsb = bufs("gsb", 4)
    b_carry = [Buf("carry%d" % l) for l in range(NL)]
    b_wr = bufs("wr", NB)
    b_cst = Buf("cst")
    b_cstb = Buf("cstb")
    b_par = Buf("par")
    b_par2 = Buf("par2")
    b_ps = [Buf("ps%d" % i, excl=True) for i in range(8)]
    b_wbf = bufs("wbf", NL * NPIECE)

    st = {"ps": 0, "tmp": 0, "sq": 0, "gt": 0}

    held = set()

    def next_ps(hold=False):
        while st["ps"] % 8 in held:
            st["ps"] += 1
        i = st["ps"] % 8
        st["ps"] += 1
        if hold:
            held.add(i)
        return psum[i], b_ps[i]

    def release_ps(ps):
        held.discard(psum.index(ps))

    def next_tmp():
        i = st["tmp"] % NTMP
        st["tmp"] += 1
        return tmpf[:, i, :], b_tmp[i]

    def next_sq():
        i = st["sq"] % 2
        st["sq"] += 1
        return sqt[:, i, :], b_sq[i]

    def next_gt(fixed=None):
        if fixed is not None:
            return gt[:, fixed, :], b_gt[fixed]
        i = 3 + st["gt"] % (NG - 3)
        st["gt"] += 1
        return gt[:, i, :], b_gt[i]

    sl_c = P.new_slot("cst")
    sl_p = P.new_slot("par")
    P.op("sp", lambda E: E.dma_start(out=cst[:], in_=cst_d), writes=[b_cst], slot=sl_c)
    P.op("sp", lambda E: E.dma_start(out=par[:], in_=par_d), writes=[b_par], slot=sl_p)
    cast_slots = [P.new_slot("cast") for _ in range(NL * NPIECE)]
    def cast_piece(i):
        P.op("pool", lambda E: E.dma_start(out=wbf[i], in_=wall[i]), writes=[b_wbf[i]], slot=cast_slots[i])

    CAST_LA = 6
    cast_state = {"n": 0}

    def emit_casts_upto(k):
        k = min(k, nlayers * NPIECE - 1)
        while cast_state["n"] <= k:
            cast_piece(cast_state["n"])
            cast_state["n"] += 1

    emit_casts_upto(CAST_LA)

    def emit_late_casts(n):
        pass

    P.op("dve", lambda E: E.tensor_copy(out=cstb[:, 0:640], in_=cst[:, 0:640]), reads=[b_cst], writes=[b_cstb])
    for l in range(NL):
        P.op("act", (lambda l: lambda E: E.activation(out=par2[:, l * 16:l * 16 + 8],
                                                      in_=par[:, l * PL + P_SINK:l * PL + P_SINK + 8], func=AF.Exp))(l),
             reads=[b_par], writes=[b_par2])
        P.op("dve", (lambda l: lambda E: E.tensor_scalar(out=par2[:, l * 16 + 8:l * 16 + 9],
                                                         in0=par[:, l * PL + P_GB + 1:l * PL + P_GB + 2],
                                                         scalar1=-1.0, scalar2=None, op0=ALU.mult))(l),
             reads=[b_par], writes=[b_par2])
        P.op("dve", (lambda l: lambda E: E.memset(Cst[l][:], 0.0))(l), writes=b_Cst[l])
        P.op("dve", (lambda l: lambda E: E.memset(carry[l][:], 0.0))(l), writes=[b_carry[l]])
        P.op("dve", (lambda l: lambda E: E.memset(vaug[l][:], 1.0))(l), writes=b_va[l])
        P.op("dve", (lambda l: lambda E: E.memset(kTb[l][:], 0.0))(l), writes=b_kT[l])
    P.op("dve", lambda E: E.memset(vmaug[:], 1.0), writes=b_vm)
    P.op("dve", lambda E: E.memset(gt[:], 0.0), writes=b_gt)
    P.op("dve", lambda E: E.memset(gsm[:], 0.0), writes=[b_gsm])
    P.op("dve", lambda E: E.memset(gones[:], 1.0), writes=[b_gones])

    P.op("dve", lambda E: E.tensor_copy(out=cstb[:, 640:768], in_=cst[:, K_IDENT:K_IDENT + 128]), reads=[b_cst], writes=[b_cstb])
    for m_, off_ in ((0, K_MOWN), (1, K_MPREV)):
        for g_ in range(4):
            P.op("dve", lambda E: E.tensor_scalar(out=maskb[:, m_, g_ * 128:(g_ + 1) * 128], in0=cst[:, off_:off_ + 128],
                                                  scalar1=-1.0, scalar2=30000.0, op0=ALU.add, op1=ALU.mult), reads=[b_cst], writes=[b_cstb])
    ident_bf = cstb[:, 640:768]
    ones_bf = cstb[:, K_ONES:K_ONES + 128]
    blk_bf = cstb[:, K_BLK:K_BLK + 128]
    rot_bf = cstb[:, K_ROT:K_ROT + 128]
    mown_bf = cstb[:, K_MOWN:K_MOWN + 128]
    mprev_bf = cstb[:, K_MPREV:K_MPREV + 128]

    ring_slots = [P.new_slot("ring") for _ in range(NB)]
    seq = [(t, l, p) for t in range(ntiles) for l in range(nlayers) for p in range(NPIECE)]
    ring = {"issued": 0}

    def piece(n):
        emit_casts_upto(n + CAST_LA)
        while ring["issued"] < min(n + NB - 1, len(seq)):
            m = ring["issued"]
            (_, l, p) = seq[m]
            s = m % NB
            P.op("sp", (lambda s, i: lambda E: E.dma_start(out=wr[:, s, :], in_=wbf[i]))(s, l * NPIECE + p),
                 reads=[b_wbf[l * NPIECE + p]], writes=[b_wr[s]], slot=ring_slots[s])
            ring["issued"] += 1
        return n % NB, b_wr[n % NB]

    x_slots = [P.new_slot("xin") for _ in range(2)]
    o_slots = [P.new_slot("xout") for _ in range(2)]
    out_ops = []

    def load_x(t):
        xb = t % 2
        P.op("sp", lambda E: E.dma_start(out=xbuf[xb][:],
                                         in_=xT[:, t * T:(t + 1) * T].rearrange("(kc p) n -> p kc n", p=128)),
             writes=b_x[xb], slot=x_slots[xb])

    def rmsnorm(xb, gcol):
        X = xbuf[xb]
        ps, bps = next_ps()
        for kc in range(KC):
            sq, bsq = next_sq()
            P.op("act", (lambda kc, sq: lambda E: E.activation(out=sq, in_=X[:, kc, :], func=AF.Square))(kc, sq),
                 reads=[b_x[xb][kc]], writes=[bsq])
            P.op("pe", (lambda kc, sq: lambda E: E.matmul(ps[:], lhsT=ones_bf, rhs=sq, start=(kc == 0), stop=(kc == KC - 1)))(kc, sq),
                 reads=[bsq, b_cstb], writes=[bps])
        rs, brs = next_tmp()
        P.op("act", lambda E: E.activation(out=rs, in_=ps[:], func=AF.Ln, scale=1.0 / D, bias=EPS), reads=[bps], writes=[brs])
        P.op("act", lambda E: E.activation(out=rs, in_=rs, func=AF.Exp, scale=-0.5), reads=[brs], writes=[brs])
        for kc in range(KC):
            P.op("dve", (lambda kc: lambda E: E.scalar_tensor_tensor(
                out=hA[:, kc, :], in0=X[:, kc, :], scalar=par[:, gcol + kc:gcol + kc + 1], in1=rs,
                op0=ALU.mult, op1=ALU.mult))(kc),
                reads=[b_x[xb][kc], brs, b_par], writes=[b_hA[kc]])

    def rope_tables(t):
        a1, ba1 = next_tmp()
        a2, ba2 = next_tmp()
        rd = [b_cst]
        P.op("dve", lambda E: E.tensor_scalar(out=a1, in0=cst[:, K_AR:K_AR + T], scalar1=float(t * T),
                                              scalar2=cst[:, K_INVF:K_INVF + 1], op0=ALU.add, op1=ALU.mult),
             reads=rd, writes=[ba1])
        for (tab, shift) in ((sinT, 0.0), (cosT, 0.25)):
            P.op("dve", (lambda shift: lambda E: E.tensor_scalar(out=a2, in0=a1, scalar1=1.0 / TWO_PI, scalar2=shift,
                                                                 op0=ALU.mult, op1=ALU.add))(shift),
                 reads=[ba1], writes=[ba2])
            k1, bk1 = next_tmp()
            P.op("dve", (lambda k1: lambda E: E.tensor_scalar(out=k1, in0=a2, scalar1=MAGIC, scalar2=None, op0=ALU.add))(k1),
                 reads=[ba2], writes=[bk1])
            P.op("dve", (lambda k1: lambda E: E.tensor_scalar(out=k1, in0=k1, scalar1=-MAGIC, scalar2=None, op0=ALU.add))(k1),
                 reads=[bk1], writes=[bk1])
            P.op("dve", (lambda k1: lambda E: E.tensor_tensor(out=k1, in0=a2, in1=k1, op=ALU.subtract))(k1),
                 reads=[ba2, bk1], writes=[bk1])
            P.op("act", (lambda k1, tab: lambda E: E.activation(out=tab[:], in_=k1, func=AF.Sin, scale=TWO_PI))(k1, tab),
                 reads=[bk1], writes=[b_rope])

    pend = []
    tk = {"n": 0, "qk": 0}

    def defer(delay, fn):
        pend.append([tk["n"] + delay, fn])

    def tick():
        tk["n"] += 1
        run = [p for p in pend if p[0] <= tk["n"]]
        for p in run:
            pend.remove(p)
        for p in run:
            p[1]()

    def flush():
        while pend:
            tick()

    def qk_process(ps, bps, l, gcol, out_ap, out_bufs):
        s_ = tk["qk"] % 2
        tk["qk"] += 1
        q32, bq = qkf[:, 2 * s_, :], b_qkf[2 * s_]
        rs, brs = qkf[:, 2 * s_ + 1, :], b_qkf[2 * s_ + 1]
        sq, bsq = qkb[:, 2 * s_, :], b_qkb[2 * s_]
        qb, bqb = qkb[:, 2 * s_ + 1, :], b_qkb[2 * s_ + 1]
        P.op("act", lambda E: E.activation(out=sq, in_=ps[:], func=AF.Square), reads=[bps], writes=[bsq])
        P.op("act", lambda E: E.activation(out=q32, in_=ps[:], func=AF.Copy), reads=[bps], writes=[bq])

        def s2():
            ps2, bps2 = next_ps()
            P.op("pe", lambda E: E.matmul(ps2[:], lhsT=blk_bf, rhs=sq, start=True, stop=True), reads=[bsq, b_cstb], writes=[bps2])
            P.op("act", lambda E: E.activation(out=rs, in_=ps2[:], func=AF.Ln, scale=1.0 / 64.0, bias=EPS), reads=[bps2], writes=[brs])
            P.op("act", lambda E: E.activation(out=rs, in_=rs, func=AF.Exp, scale=-0.5), reads=[brs], writes=[brs])
            P.op("dve", lambda E: E.scalar_tensor_tensor(out=q32, in0=q32, scalar=par[:, gcol:gcol + 1], in1=rs,
                                                         op0=ALU.mult, op1=ALU.mult), reads=[bq, brs, b_par], writes=[bq])
            P.op("act", lambda E: E.activation(out=qb, in_=q32, func=AF.Copy), reads=[bq], writes=[bqb])

            def s3():
                ps3, bps3 = next_ps()
                P.op("pe", lambda E: E.matmul(ps3[:], lhsT=rot_bf, rhs=qb, start=True, stop=True), reads=[bqb, b_cstb], writes=[bps3])
                P.op("dve", lambda E: E.tensor_tensor(out=rs, in0=ps3[:], in1=sinT[:], op=ALU.mult), reads=[bps3, b_rope], writes=[brs])
                P.op("pool", lambda E: E.tensor_tensor(out=q32, in0=q32, in1=cosT[:], op=ALU.mult), reads=[bq, b_rope], writes=[bq])
                P.op("dve", lambda E: E.tensor_tensor(out=out_ap, in0=q32, in1=rs, op=ALU.add), reads=[bq, brs], writes=out_bufs)
            defer(1, s3)
        defer(1, s2)

    def dense_chunk(n, ci, rhs_list, rhs_bufs, evac):
        s, bw = piece(n)
        nk = len(rhs_list)
        ps, bps = next_ps()
        for kc in range(nk):
            P.op("pe", lambda E: E.matmul(
                ps[:], lhsT=wr[:, s, kc * 512 + ci * 128:kc * 512 + ci * 128 + 128], rhs=rhs_list[kc],
                start=(kc == 0), stop=(kc == nk - 1)),
                reads=[bw, rhs_bufs[kc]], writes=[bps])
        tick()
        evac(ci, ps, bps)

    def dense_fm(n, nchunks, rhs_list, rhs_bufs, evac):
        for ci in range(nchunks):
            dense_chunk(n, ci, rhs_list, rhs_bufs, evac)

    cur = {}

    def chk(k):
        if stage == k:
            raise _Stop()

    def main_loop():
        load_x(0)
        for t in range(ntiles):
            xb = t % 2
            cur['t'], cur['xb'] = t, xb
            chk(0)
            if t + 1 < ntiles:
                load_x(t + 1)
            rope_tables(t)
            for l in range(nlayers):
                base = (t * nlayers + l) * NPIECE
                pc = l * PL
                hrhs = [hA[:, kc, :] for kc in range(KC)]
                rmsnorm(xb, pc + P_GMIX)
                chk(1)
                emit_late_casts(8)

                gI = [None]
                gF = [None]

                def evac_win(cidx):
                    def f(ci, ps, bps):
                        c = cidx + ci
                        import os
                        dbg = os.environ.get("KDBG", "qifms")
                        kind = "q" if c <= C_KA else "i" if c == C_IFI else "f" if c == C_IFF else "m" if c < C_OM else "s"
                        if kind not in dbg:
                            P.op("act", lambda E: E.activation(out=R[:, c, :], in_=ps[:], func=AF.Copy), reads=[bps], writes=[b_R[c]])
                            return
                        if c < 4:
                            qk_process(ps, bps, l, pc + P_GQ, R[:, c, :], [b_R[c]])
                        elif c == C_KA:
                            qk_process(ps, bps, l, pc + P_GK, kTb[l][:, 128:128 + T], [b_kT[l][1]])
                        elif c == C_IFI:
                            g, bg = next_gt(0)
                            gI[0] = (g, bg)
                            P.op("dve", lambda E: E.tensor_scalar(out=g[0:4, :], in0=ps[0:4, :], scalar1=par[0:4, pc + P_GB:pc + P_GB + 1],
                                                                  scalar2=None, op0=ALU.add), reads=[bps, b_par], writes=[bg])
                        elif c == C_IFF:
                            g, bg = next_gt(1)
                            gF[0] = (g, bg)
                            P.op("act", lambda E: E.activation(out=g[0:4, :], in_=ps[0:4, :], func=AF.Exp, scale=-1.0,
                                                               bias=par2[0:4, l * 16 + 8:l * 16 + 9]), reads=[bps, b_par2], writes=[bg])
                        elif c in (C_QM, C_QM + 1):
                            P.op("act", lambda E: E.activation(out=R[:, c, :], in_=ps[:], func=AF.Copy, scale=0.125), reads=[bps], writes=[b_R[c]])
                        elif c in (C_KM, C_KM + 1):
                            P.op("dve", lambda E: E.tensor_copy(out=R[:, c, :], in_=ps[:]), reads=[bps], writes=[b_R[c]])
                        elif c < 31:
                            sg, bsg = next_tmp()
                            P.op("act", lambda E: E.activation(out=sg, in_=ps[:], func=AF.Exp, scale=-1.0), reads=[bps], writes=[bsg])
                            P.op("act", lambda E: E.activation(out=sg, in_=sg, func=AF.Ln, bias=1.0), reads=[bsg], writes=[bsg])
                            P.op("act", lambda E: E.activation(out=R[:, c, :], in_=sg, func=AF.Exp, scale=-1.0), reads=[bsg], writes=[b_R[c]])
                    return f

                for pp in range(3):
                    dense_fm(base + pp, 4, hrhs, b_hA, evac_win(pp * 4))
                dn_steps = []
                for pp in range(3, 8):
                    for ci in range(4 if pp < 7 else 3):
                        dn_steps.append((lambda pp, ci: lambda: dense_chunk(base + pp + 2, ci, hrhs, b_hA, evac_win(pp * 4)))(pp, ci))

                flush()
                chk(2)
                s8, bw8 = piece(base + 3)
                for sub in range(4):
                    ps, bps = next_ps()
                    for kc in range(KC):
                        P.op("pe", (lambda sub, kc, ps: lambda E: E.matmul(
                            ps[:], lhsT=hA[:, kc, sub * 128:(sub + 1) * 128], rhs=wr[:, s8, kc * 512:(kc + 1) * 512],
                            start=(kc == 0), stop=(kc == KC - 1)))(sub, kc, ps),
                            reads=[bw8, b_hA[kc]], writes=[bps])
                    P.op("act", (lambda sub, ps: lambda E: E.activation(
                        out=vmaug[:, sub, :, 0:128], in_=ps[:].rearrange("p (h v) -> p h v", h=4), func=AF.Copy))(sub, ps),
                        reads=[bps], writes=[b_vm[sub]])
                s9, bw9 = piece(base + 4)
                for sub in range(4):
                    ps, bps = next_ps()
                    for kc in range(KC):
                        P.op("pe", (lambda sub, kc, ps: lambda E: E.matmul(
                            ps[:, 0:384], lhsT=hA[:, kc, sub * 128:(sub + 1) * 128], rhs=wr[:, s9, kc * 512:kc * 512 + 384],
                            start=(kc == 0), stop=(kc == KC - 1)))(sub, kc, ps),
                            reads=[bw9, b_hA[kc]], writes=[bps])
                    P.op("dve", (lambda sub, ps: lambda E: E.tensor_copy(out=vaug[l][:, 1 + sub, 0, 0:64], in_=ps[:, 0:64]))(sub, ps),
                         reads=[bps], writes=[b_va[l][1 + sub]])
                    P.op("dve", (lambda sub, ps: lambda E: E.tensor_copy(out=vaug[l][:, 1 + sub, 1, 64:128], in_=ps[:, 64:128]))(sub, ps),
                         reads=[bps], writes=[b_va[l][1 + sub]])
                    P.op("act", (lambda sub, ps: lambda E: E.activation(out=ktok[:, sub, :], in_=ps[:, 128:384], func=AF.Copy))(sub, ps),
                         reads=[bps], writes=[b_kt[sub]])

                chk(3)
                def att_scores(b, j):
                    gb = t * 4 + b
                    nr = slice(j * 64, j * 64 + 64)
                    kbs = ([(b, 1)] if gb > 0 else []) + [(b + 1, 0)]
                    pts = []
                    for ki, (blk, mi) in enumerate(kbs):
                        ps, bps = next_ps()
                        P.op("pe", lambda E: E.matmul(ps[:], lhsT=ident_bf, rhs=maskb[:, mi, :], start=True, stop=False),
                             reads=[b_cstb], writes=[bps])
                        for g in range(4):
                            P.op("pe", lambda E: E.matmul(
                                ps[:, g * 128:(g + 1) * 128], lhsT=kTb[l][nr, blk * 128:(blk + 1) * 128],
                                rhs=R[nr, g, b * 128:(b + 1) * 128], start=False, stop=(g == 3)),
                                reads=[b_kT[l][0 if blk == 0 else 1], b_R[g]], writes=[bps])
                        pi = (j * 2 + ki)
                        P.op("act", lambda E: E.activation(out=PT[:, pi, :], in_=ps[:], func=AF.Exp, scale=0.125),
                             reads=[bps], writes=[b_PT[pi]])
                        pts.append((pi, blk))
                    return pts

                def att_pv(b, j, pts):
                    nr = slice(j * 64, j * 64 + 64)
                    dr = slice((1 - j) * 64, (1 - j) * 64 + 64)
                    pv, bpv = next_ps()
                    for ki, (pi, blk) in enumerate(pts):
                        P.op("pe", lambda E: E.matmul(
                            pv[:], lhsT=vaug[l][:, blk, j, :], rhs=PT[:, pi, :], start=(ki == 0), stop=(ki == len(pts) - 1)),
                            reads=[b_va[l][blk], b_PT[pi]], writes=[bpv])
                    dt_, bdt = next_tmp()
                    P.op("dve", lambda E: E.tensor_tensor(
                        out=dt_[nr, :].rearrange("p (g q) -> p g q", g=4), in0=pv[dr, :].rearrange("p (g q) -> p g q", g=4),
                        in1=par2[dr, l * 16 + j * 4:l * 16 + j * 4 + 4].unsqueeze(2).to_broadcast([64, 4, 128]), op=ALU.add),
                        reads=[bpv, b_par2], writes=[bdt])
                    P.op("act", lambda E: E.activation(out=dt_[nr, :], in_=dt_[nr, :], func=AF.Ln), reads=[bdt], writes=[bdt])
                    P.op("act", lambda E: E.activation(out=dt_[nr, :], in_=dt_[nr, :], func=AF.Exp, scale=-1.0), reads=[bdt], writes=[bdt])
                    P.op("dve", lambda E: E.tensor_tensor(
                        out=attT[nr, :, b * 128:(b + 1) * 128], in0=pv[nr, :].rearrange("p (g q) -> p g q", g=4),
                        in1=dt_[nr, :].rearrange("p (g q) -> p g q", g=4), op=ALU.mult),
                        reads=[bpv, bdt], writes=[b_att[b][j]])

                at_steps = []
                at_state = {}
                items = [(b, j) for b in range(4) for j in range(2)]
                for ii, (b, j) in enumerate(items):
                    at_steps.append((lambda b, j: lambda: at_state.__setitem__((b, j), att_scores(b, j)))(b, j))
                    if ii > 0:
                        pb, pj = items[ii - 1]
                        at_steps.append((lambda pb, pj: lambda: att_pv(pb, pj, at_state[(pb, pj)]))(pb, pj))
                pb, pj = items[-1]
                at_steps.append((lambda pb, pj: lambda: att_pv(pb, pj, at_state[(pb, pj)]))(pb, pj))

                def att_tail():
                    P.op("pool", lambda E: E.tensor_copy(out=kTb[l][:, 0:128], in_=kTb[l][:, T:T + 128]), reads=[b_kT[l][1]], writes=[b_kT[l][0]])
                    P.op("pool", lambda E: E.tensor_copy(out=vaug[l][:, 0, :, :], in_=vaug[l][:, 4, :, :]), reads=[b_va[l][4]], writes=[b_va[l][0]])

                chk(4)
                gi, bgi = gI[0]
                ge, bge = gF[0]
                cr = carry[l]
                P.op("act", lambda E: E.activation(out=ge[0:4, :], in_=ge[0:4, :], func=AF.Ln, bias=1.0), reads=[bge], writes=[bge])
                Bg, bBg = next_gt(2)
                P.op("dve", lambda E: E.tensor_tensor_scan(out=Bg[0:4, :], data0=gones[0:4, :],
                                                           data1=ge[0:4, :], initial=cr[0:4, 0:1], op0=ALU.mult, op1=ALU.subtract),
                     reads=[bge, b_carry[l], b_gones], writes=[bBg])
                P.op("dve", lambda E: E.tensor_copy(out=cr[0:4, 0:1], in_=Bg[0:4, T - 1:T]), reads=[bBg], writes=[b_carry[l]])
                P.op("dve", lambda E: E.tensor_tensor(out=gi[0:4, :], in0=gi[0:4, :], in1=Bg[0:4, :], op=ALU.subtract), reads=[bgi, bBg], writes=[bgi])
                P.op("dve", lambda E: E.tensor_copy(out=gsm[0:4, 0:1], in_=cr[0:4, 1:2]), reads=[b_carry[l]], writes=[b_gsm])
                P.op("dve", lambda E: E.tensor_reduce(out=gsm[0:4, 8:12], in_=gi[0:4, :].rearrange("p (c s) -> p c s", c=4), axis=AX.X, op=ALU.max),
                     reads=[bgi, b_gsm], writes=[b_gsm])
                P.op("dve", lambda E: E.tensor_tensor_scan(out=gsm[0:4, 1:5], data0=gsm[0:4, 8:12], data1=gsm[0:4, 8:12], initial=gsm[0:4, 0:1],
                                                           op0=ALU.max, op1=ALU.max), reads=[b_gsm], writes=[b_gsm])
                P.op("dve", lambda E: E.tensor_copy(out=cr[0:4, 1:2], in_=gsm[0:4, 4:5]), reads=[b_gsm], writes=[b_carry[l]])
                P.op("dve", lambda E: E.tensor_tensor(out=gsm[0:4, 24:28], in0=gsm[0:4, 0:4], in1=gsm[0:4, 1:5], op=ALU.subtract), reads=[b_gsm], writes=[b_gsm])
                P.op("act", lambda E: E.activation(out=gsm[0:4, 16:20], in_=gsm[0:4, 24:28], func=AF.Exp), reads=[b_gsm], writes=[b_gsm])
                Mb = gsm[0:4, 1:5].unsqueeze(2).to_broadcast([4, 4, 128])
                P.op("dve", lambda E: E.tensor_tensor(out=gi[0:4, :].rearrange("p (c s) -> p c s", c=4), in0=gi[0:4, :].rearrange("p (c s) -> p c s", c=4),
                                                      in1=Mb, op=ALU.subtract), reads=[bgi, b_gsm], writes=[bgi])
                P.op("act", lambda E: E.activation(out=gi[0:4, :], in_=gi[0:4, :], func=AF.Exp), reads=[bgi], writes=[bgi])
                P.op("dve", lambda E: E.scalar_tensor_tensor(out=Bg[0:4, :].rearrange("p (c s) -> p c s", c=4), in0=Bg[0:4, :].rearrange("p (c s) -> p c s", c=4),
                                                             scalar=-1.0, in1=Mb, op0=ALU.mult, op1=ALU.subtract), reads=[bBg, b_gsm], writes=[bBg])
                P.op("act", lambda E: E.activation(out=Bg[0:4, :], in_=Bg[0:4, :], func=AF.Exp), reads=[bBg], writes=[bBg])

                chk(5)
                mst = {}

                def mA(c):
                    cc = slice(c * 128, (c + 1) * 128)
                    wi = c % 2
                    td, btd = next_gt()
                    P.op("dve", lambda E: E.tensor_tensor(
                        out=td[0:4, :].rearrange("p (h s) -> p h s", h=4), in0=cst[0:4, K_DIAG:K_DIAG + 512].rearrange("p (h s) -> p h s", h=4),
                        in1=Bg[0:4, cc].unsqueeze(1).to_broadcast([4, 4, 128]), op=ALU.mult), reads=[bBg, b_cst], writes=[btd])
                    P.op("dve", lambda E: E.tensor_scalar(out=gsm[0:4, 32:36], in0=cst[0:4, K_ID4:K_ID4 + 4], scalar1=gsm[0:4, 16 + c:17 + c],
                                                          scalar2=None, op0=ALU.mult), reads=[b_gsm, b_cst], writes=[b_gsm])
                    psg, bpsg = next_ps()
                    P.op("pe", lambda E: E.matmul(psg[:, 0:4], lhsT=gi[0:4, cc], rhs=cst[0:4, K_ID4:K_ID4 + 4], start=True, stop=True),
                         reads=[bgi, b_cst], writes=[bpsg])
                    P.op("pe", lambda E: E.matmul(psg[:, 4:8], lhsT=cst[0:4, K_ONES:K_ONES + 128], rhs=gsm[0:4, 32:36], start=True, stop=True),
                         reads=[b_gsm, b_cst], writes=[bpsg])
                    P.op("dve", lambda E: E.tensor_copy(out=gsb[:, c, :], in_=psg[:, 0:8]), reads=[bpsg], writes=[b_gsb[c]])
                    psS2 = [next_ps(), next_ps()]
                    for h in range(4):
                        hp = slice((h % 2) * 64, (h % 2) * 64 + 64)
                        psS, bpsS = psS2[h % 2]
                        co = (h // 2) * 128
                        P.op("pe", lambda E: E.matmul(
                            psS[:, co:co + 128], lhsT=R[hp, C_KM + h // 2, cc], rhs=R[hp, C_QM + h // 2, cc], start=True, stop=True),
                            reads=[b_R[C_KM + h // 2], b_R[C_QM + h // 2]], writes=[bpsS])
                    for h in range(4):
                        psS, bpsS = psS2[h % 2]
                        co = (h // 2) * 128
                        P.op("dve", lambda E: E.scalar_tensor_tensor(
                            out=WT[:, wi, h, :], in0=psS[:, co:co + 128], scalar=gsb[:, c, h:h + 1], in1=cst[:, K_MOWN:K_MOWN + 128],
                            op0=ALU.mult, op1=ALU.mult), reads=[bpsS, b_gsb[c], b_cst], writes=[b_WT[wi]])
                        P.op("pool", lambda E: E.tensor_scalar(
                            out=kp[:, wi, h, :], in0=ktok[:, c, h * 64:(h + 1) * 64], scalar1=gsb[:, c, h:h + 1], scalar2=None, op0=ALU.mult),
                            reads=[b_kt[c], b_gsb[c]], writes=[b_kp[wi]])
                    mst[c] = {"td": (td, btd)}

                def mB(c):
                    cc = slice(c * 128, (c + 1) * 128)
                    wi = c % 2
                    for h in range(4):
                        hp = slice((h % 2) * 64, (h % 2) * 64 + 64)
                        P.op("dve", lambda E: E.tensor_scalar(out=Cbf[hp, h, :], in0=Cst[l][hp, h, :], scalar1=gsb[hp, c, 4 + h:5 + h],
                                                              scalar2=None, op0=ALU.mult),
                             reads=[b_Cst[l][h], b_gsb[c]], writes=[b_Cbf[h]])
                    psN, bpsN = next_ps(hold=True)
                    psD, bpsD = next_ps(hold=True)
                    for (pq, bpq, off) in ((psN, bpsN, 0), (psD, bpsD, 128)):
                        for h in range(4):
                            hp = slice((h % 2) * 64, (h % 2) * 64 + 64)
                            P.op("pe", lambda E: E.matmul(
                                pq[:, h * 128:(h + 1) * 128], lhsT=vmaug[:, c, h, off:off + 128], rhs=WT[:, wi, h, :], start=True, stop=False),
                                reads=[b_vm[c], b_WT[wi]], writes=[bpq])
                            P.op("pe", lambda E: E.matmul(
                                pq[:, h * 128:(h + 1) * 128], lhsT=Cbf[hp, h, off:off + 128], rhs=R[hp, C_QM + h // 2, cc], start=False, stop=True),
                                reads=[b_Cbf[h], b_R[C_QM + h // 2]], writes=[bpq])
                    for hh in range(2):
                        psC, bpsC = next_ps()
                        for h2 in range(2):
                            h = hh * 2 + h2
                            P.op("pe", lambda E: E.matmul(
                                psC[0:64, h2 * 256:(h2 + 1) * 256], lhsT=kp[:, wi, h, :], rhs=vmaug[:, c, h, :], start=True, stop=True),
                                reads=[b_kp[wi], b_vm[c]], writes=[bpsC])
                        for h2 in range(2):
                            h = hh * 2 + h2
                            hp = slice((h % 2) * 64, (h % 2) * 64 + 64)
                            P.op("dve", lambda E: E.scalar_tensor_tensor(
                                out=Cst[l][hp, h, :], in0=Cst[l][hp, h, :], scalar=gsb[hp, c, 4 + h:5 + h], in1=psC[0:64, h2 * 256:(h2 + 1) * 256],
                                op0=ALU.mult, op1=ALU.add), reads=[b_Cst[l][h], b_gsb[c], bpsC], writes=[b_Cst[l][h]])
                    mst[c]["nd"] = (psN, bpsN, psD, bpsD)

                def mC(c):
                    cc = slice(c * 128, (c + 1) * 128)
                    td, btd = mst[c]["td"]
                    psN, bpsN, psD, bpsD = mst[c]["nd"]
                    pst, bpst = next_ps()
                    P.op("pe", lambda E: E.matmul(pst[:], lhsT=cst[0:4, K_ONES:K_ONES + 128], rhs=td[0:4, :], start=True, stop=True),
                         reads=[btd, b_cst], writes=[bpst])
                    thrb, bthr = next_tmp()
                    P.op("dve", lambda E: E.tensor_copy(out=thrb, in_=pst[:]), reads=[bpst], writes=[bthr])
                    ad, bad = next_tmp()
                    P.op("act", lambda E: E.activation(out=ad, in_=psD[:], func=AF.Abs), reads=[bpsD], writes=[bad])
                    P.op("dve", lambda E: E.tensor_tensor(out=ad, in0=ad, in1=thrb, op=ALU.max), reads=[bad, bthr], writes=[bad])
                    P.op("act", lambda E: E.activation(out=ad, in_=ad, func=AF.Ln), reads=[bad], writes=[bad])
                    P.op("act", lambda E: E.activation(out=ad, in_=ad, func=AF.Exp, scale=-1.0), reads=[bad], writes=[bad])
                    P.op("dve", lambda E: E.tensor_tensor(out=ad, in0=psN[:], in1=ad, op=ALU.mult), reads=[bpsN, bad], writes=[bad])
                    sq, bsq = next_sq()
                    P.op("act", lambda E: E.activation(out=sq, in_=ad, func=AF.Square), reads=[bad], writes=[bsq])
                    pss, bpss = next_ps()
                    P.op("pe", lambda E: E.matmul(pss[:], lhsT=ones_bf, rhs=sq, start=True, stop=True), reads=[bsq, b_cstb], writes=[bpss])
                    rs, brs = next_tmp()
                    P.op("act", lambda E: E.activation(out=rs, in_=pss[:], func=AF.Ln, scale=1.0 / 128.0, bias=EPS), reads=[bpss], writes=[brs])
                    P.op("act", lambda E: E.activation(out=rs, in_=rs, func=AF.Exp, scale=-0.5), reads=[brs], writes=[brs])
                    P.op("pool", lambda E: E.tensor_tensor(out=ad, in0=ad, in1=rs, op=ALU.mult), reads=[bad, brs], writes=[bad])
                    P.op("pool", lambda E: E.tensor_tensor(
                        out=ad.rearrange("p (h s) -> p h s", h=4), in0=ad.rearrange("p (h s) -> p h s", h=4),
                        in1=par[:, pc + P_MHN:pc + P_MHN + 4].unsqueeze(2).to_broadcast([128, 4, 128]), op=ALU.mult), reads=[bad, b_par], writes=[bad])
                    P.op("pool", lambda E: E.tensor_tensor(
                        out=hmT[:, :, cc], in0=ad.rearrange("p (h s) -> p h s", h=4), in1=R[:, C_OM:C_OM + 4, cc], op=ALU.mult),
                        reads=[bad] + b_R[C_OM:C_OM + 4], writes=[b_hm[c]])
                    release_ps(psN)
                    release_ps(psD)

                ml_steps = []
                for (stg, c) in (("A", 0), ("A", 1), ("B", 0), ("A", 2), ("C", 0), ("B", 1), ("A", 3), ("C", 1), ("B", 2), ("C", 2), ("B", 3), ("C", 3)):
                    ml_steps.append((lambda stg, c: lambda: {"A": mA, "B": mB, "C": mC}[stg](c))(stg, c))
                merged = []
                for lst in (at_steps, ml_steps, dn_steps):
                    for i_, fn_ in enumerate(lst):
                        merged.append(((i_ + 0.5) / len(lst), len(merged), fn_))
                merged.sort(key=lambda x: (x[0], x[1]))
                emit_late_casts(6)
                for k_, (_, _, fn_) in enumerate(merged):
                    fn_()
                    if k_ % 4 == 3:
                        emit_late_casts(1)
                att_tail()
                flush()
                chk(6)
                sA, bwA = piece(base + 10)
                sM, bwM = piece(base + 11)
                att_reads = [x for bb in b_att for x in bb]
                for oc in range(8):
                    psA, bpsA = next_ps()
                    for g in range(4):
                        P.op("pe", (lambda oc, g, psA: lambda E: E.matmul(psA[:], lhsT=wr[:, sA, g * 1024 + oc * 128:g * 1024 + oc * 128 + 128], rhs=attT[:, g, :],
                                                                          start=(g == 0), stop=(g == 3)))(oc, g, psA), reads=[bwA] + att_reads, writes=[bpsA])
                    psB, bpsB = next_ps()
                    for h in range(4):
                        P.op("pe", (lambda oc, h, psB: lambda E: E.matmul(psB[:], lhsT=wr[:, sM, h * 1024 + oc * 128:h * 1024 + oc * 128 + 128], rhs=hmT[:, h, :],
                                                                          start=(h == 0), stop=(h == 3)))(oc, h, psB), reads=[bwM] + b_hm, writes=[bpsB])
                    t1, bt1 = next_tmp()
                    t2, bt2 = next_tmp()
                    P.op("dve", (lambda oc, psA, t1: lambda E: E.tensor_tensor(out=t1, in0=psA[:], in1=R[:, C_GA + oc, :], op=ALU.mult))(oc, psA, t1),
                         reads=[bpsA, b_R[C_GA + oc]], writes=[bt1])
                    P.op("dve", (lambda oc, psB, t2: lambda E: E.tensor_tensor(out=t2, in0=psB[:], in1=R[:, C_GM + oc, :], op=ALU.mult))(oc, psB, t2),
                         reads=[bpsB, b_R[C_GM + oc]], writes=[bt2])
                    kbr = os.environ.get("KBR", "")
                    if kbr == "a":
                        P.op("dve", (lambda oc, t1, t2: lambda E: E.tensor_copy(out=hA[:, oc, :], in_=t1))(oc, t1, t2), reads=[bt1, bt2], writes=[b_hA[oc]])
                    elif kbr == "m":
                        P.op("dve", (lambda oc, t1, t2: lambda E: E.tensor_copy(out=hA[:, oc, :], in_=t2))(oc, t1, t2), reads=[bt1, bt2], writes=[b_hA[oc]])
                    else:
                        P.op("pool", (lambda oc, t1, t2: lambda E: E.tensor_tensor(out=hA[:, oc, :], in0=t1, in1=t2, op=ALU.add))(oc, t1, t2),
                             reads=[bt1, bt2], writes=[b_hA[oc]])

                chk(7)
                X = xbuf[xb]
                for half in range(2):
                    def evac_out(ci, ps, bps, half=half):
                        oc = half * 4 + ci
                        P.op("dve", lambda E: E.tensor_tensor(out=X[:, oc, :], in0=X[:, oc, :], in1=ps[:], op=ALU.add),
                             reads=[bps, b_x[xb][oc]], writes=[b_x[xb][oc]])
                    dense_fm(base + 12 + half, 4, hrhs, b_hA, evac_out)

                chk(8)
                rmsnorm(xb, pc + P_GFFN)
                for pp in range(8):
                    def evac_ff1(ci, ps, bps, pp=pp):
                        c = pp * 4 + ci
                        tr, btr = next_tmp()
                        P.op("act", lambda E: E.activation(out=tr, in_=ps[:], func=AF.Relu), reads=[bps], writes=[btr])
                        P.op("pool" if (c % 2 == 1) else "dve", lambda E: E.tensor_tensor(out=R[:, c, :], in0=tr, in1=tr, op=ALU.mult), reads=[btr], writes=[b_R[c]])
                    dense_fm(base + 14 + pp, 4, hrhs, b_hA, evac_ff1)
                last = (l == nlayers - 1)
                for oc in range(8):
                    s, bw = piece(base + 22 + oc)
                    ps, bps = next_ps()
                    for kc in range(32):
                        P.op("pe", (lambda kc, ps, s: lambda E: E.matmul(ps[:], lhsT=wr[:, s, kc * 128:(kc + 1) * 128], rhs=R[:, kc, :],
                                                                         start=(kc == 0), stop=(kc == 31)))(kc, ps, s), reads=[bw, b_R[kc]], writes=[bps])
                    P.op("dve", (lambda oc, ps: lambda E: E.tensor_tensor(out=X[:, oc, :], in0=X[:, oc, :], in1=ps[:], op=ALU.add))(oc, ps),
                         reads=[bps, b_x[xb][oc]], writes=[b_x[xb][oc]])
            out_ops.append(P.op("sp", (lambda t, xb: lambda E: E.dma_start(
                out=yT[:, t * T:(t + 1) * T].rearrange("(kc p) n -> p kc n", p=128), in_=xbuf[xb][:]))(t, xb),
                reads=b_x[xb], slot=o_slots[xb]))


    try:
        main_loop()
    except _Stop:
        t_, xb_ = cur['t'], cur['xb']
        out_ops.append(P.op('sp', lambda E: E.dma_start(out=yT[:, t_ * T:(t_ + 1) * T].rearrange('(kc p) n -> p kc n', p=128), in_=xbuf[xb_][:]), reads=b_x[xb_], slot=o_slots[xb_]))

    P.emit_all(out_ops)
    es.close()
    return nc


def _consts():
    c = np.zeros((128, NCST), np.float32)
    c[:, K_ONES:K_ONES + 128] = 1.0
    p = np.arange(128)
    c[:, K_BLK:K_BLK + 128] = (p[:, None] // 64 == p[None, :] // 64)
    rot = np.zeros((128, 128), np.float32)
    for hb in (0, 64):
        for d in range(8):
            rot[hb + d + 8, hb + d] = -1.0
            rot[hb + d, hb + d + 8] = 1.0
    c[:, K_ROT:K_ROT + 128] = rot
    c[:, K_MOWN:K_MOWN + 128] = (p[:, None] <= p[None, :])
    c[:, K_MPREV:K_MPREV + 128] = (p[:, None] > p[None, :])
    c[0:4, K_ID4:K_ID4 + 4] = np.eye(4, dtype=np.float32)
    d = p % 64
    invf = np.where(d < 16, (500000.0 ** (-(2.0 * (d % 8)) / 16.0)), 0.0)
    c[:, K_INVF] = invf.astype(np.float32)
    c[:, K_AR:K_AR + T] = np.arange(T, dtype=np.float32)[None, :]
    dg = np.zeros((4, 4, 128), np.float32)
    for h in range(4):
        dg[h, h, :] = 1.0
    c[0:4, K_DIAG:K_DIAG + 512] = dg.reshape(4, 512)
    c[:, K_IDENT:K_IDENT + 128] = np.eye(128, dtype=np.float32)
    return c


def _pieces(w_in, w_att, w_m, w_out, w1, w2):
    out = np.zeros((NPIECE, 128, 4096), np.float32)
    o_qa, o_ka, o_va, o_qm, o_km, o_vm, o_om, o_if, o_ga, o_gm = 0, 512, 640, 768, 1024, 1280, 1792, 2304, 2312, 3336
    cols = np.full((32, 128), -1, np.int64)
    for c in range(4):
        cols[C_QA + c, 0:64] = o_qa + c * 64 + np.arange(64)
        cols[C_QA + c, 64:128] = o_qa + (4 + c) * 64 + np.arange(64)
    cols[C_KA] = o_ka + np.arange(128)
    cols[C_IFI, 0:4] = o_if + np.arange(4)
    cols[C_IFF, 0:4] = o_if + 4 + np.arange(4)
    for c in range(2):
        cols[C_QM + c] = o_qm + c * 128 + np.arange(128)
        cols[C_KM + c] = o_km + c * 128 + np.arange(128)
    for c in range(4):
        cols[C_OM + c] = o_om + c * 128 + np.arange(128)
    for c in range(8):
        cols[C_GA + c] = o_ga + c * 128 + np.arange(128)
        cols[C_GM + c] = o_gm + c * 128 + np.arange(128)
    wpad = np.concatenate([w_in, np.zeros((D, 1), np.float32)], axis=1)
    fm = wpad[:, cols.reshape(-1)]
    fm = fm.reshape(KC, 128, 8, 512).transpose(2, 1, 0, 3)
    fm = fm.reshape(8, 128, 4096)
    out[0:3] = fm[0:3]
    out[5:10] = fm[3:8]
    out[3] = w_in[:, o_vm:o_vm + 512].reshape(KC, 128, 512).transpose(1, 0, 2).reshape(128, 4096)
    tm = np.zeros((D, 512), np.float32)
    tm[:, 0:128] = w_in[:, o_va:o_va + 128]
    tm[:, 128:384] = w_in[:, o_km:o_km + 256]
    out[4] = tm.reshape(KC, 128, 512).transpose(1, 0, 2).reshape(128, 4096)
    rows = np.zeros((4, 128), np.int64)
    for g in range(4):
        for j in range(2):
            rows[g, j * 64:(j + 1) * 64] = (j * 4 + g) * 64 + np.arange(64)
    out[10] = w_att[rows.reshape(-1)].reshape(4, 128, 1024).transpose(1, 0, 2).reshape(128, 4096)
    out[11] = w_m.reshape(4, 128, 1024).transpose(1, 0, 2).reshape(128, 4096)
    for half in range(2):
        out[12 + half] = w_out[:, half * 512:(half + 1) * 512].reshape(KC, 128, 512).transpose(1, 0, 2).reshape(128, 4096)
    for pp in range(8):
        out[14 + pp] = w1[:, pp * 512:(pp + 1) * 512].reshape(KC, 128, 512).transpose(1, 0, 2).reshape(128, 4096)
    for oc in range(8):
        out[22 + oc] = w2[:, oc * 128:(oc + 1) * 128].reshape(32, 128, 128).transpose(1, 0, 2).reshape(128, 4096)
    return out


def _params(norm_mix, norm_ffn, att_q_norm, att_k_norm, att_sinks, m_gate_bias, m_head_norm):
    par = np.zeros((128, NL * PL), np.float32)
    p = np.arange(128)
    for l in range(NL):
        b = l * PL
        par[:, b + P_GMIX:b + P_GMIX + 8] = norm_mix[l].reshape(8, 128).T
        par[:, b + P_GFFN:b + P_GFFN + 8] = norm_ffn[l].reshape(8, 128).T
        par[:, b + P_GQ] = att_q_norm[l][p % 64]
        par[:, b + P_GK] = att_k_norm[l][p % 64]
        par[:, b + P_SINK:b + P_SINK + 8] = att_sinks[l][None, :]
        par[0:4, b + P_GB] = m_gate_bias[l][0:4]
        par[0:4, b + P_GB + 1] = m_gate_bias[l][4:8]
        par[:, b + P_MHN:b + P_MHN + 4] = m_head_norm[l].reshape(4, 128).T
    return par


_NC_CACHE = {}


def kernel(x, norm_mix, w_in, att_q_norm, att_k_norm, att_sinks, m_gate_bias, m_head_norm,
           w_att_branch, w_m_branch, w_out, norm_ffn, w_ff1, w_ff2):
    f = lambda a: np.asarray(a, dtype=np.float32)
    x = f(x)
    B = x.shape[0]
    wall = np.concatenate([_pieces(f(w_in[l]), f(w_att_branch[l]), f(w_m_branch[l]), f(w_out[l]), f(w_ff1[l]), f(w_ff2[l]))
                           for l in range(NL)], axis=0)
    par = _params(f(norm_mix), f(norm_ffn), f(att_q_norm), f(att_k_norm), f(att_sinks), f(m_gate_bias), f(m_head_norm))
    cst = _consts()
    if "nc" not in _NC_CACHE:
        _NC_CACHE["nc"] = build_program()
    nc = _NC_CACHE["nc"]
    in_maps = [{"xT": np.ascontiguousarray(x[b].T), "wall": wall, "par": par, "cst": cst} for b in range(B)]
    res = run_bass_kernel_spmd(nc, in_maps, core_ids=list(range(B)))
    out = np.stack([np.ascontiguousarray(res.results[b]["yT"].T) for b in range(B)], axis=0)
    return out.astype(np.float32)
```
